# Optimizing a Trainium2 kernel written in Bass

```python
import math
import jax, jax.numpy as jnp
from jax import lax
import numpy as np

D_MODEL = 1024
BATCH = 2
SEQ = 8192
DEPTH = 4

N_HEADS_A = 4
HD_A = 64
WIDTH_A = N_HEADS_A * 2 * HD_A
DIL_PAIRS = ((128, 1), (512, 4), (2048, 16))
N_DIL = len(DIL_PAIRS)
N_HEADS_B = 4
HD_B = 128
WIDTH_B = N_HEADS_B * HD_B
SSM_GROUP = 16
SSM_STATE = 64
WIDTH_C = 512
N_GROUPS_C = WIDTH_C // SSM_GROUP
N_BRANCHES = 3
A_QK_COLS = 2 * N_HEADS_A * HD_A
B_QKV_COLS = N_DIL * N_HEADS_B * HD_B
IN_SIZES = (A_QK_COLS, A_QK_COLS, WIDTH_A, B_QKV_COLS, B_QKV_COLS, B_QKV_COLS, WIDTH_C, N_BRANCHES * D_MODEL)
IN_COLS = sum(IN_SIZES)
D_FF = 2816
CONV_WIDTH = 3
ROPE_THETA = 500000.0
ROPE_FRACTION = 4
Q_BLOCK = 128
EPS = 1e-6

kernel_name = "hybrid_gated_diffattn_dilated_s5_convffn"


def rms_norm(x, g):
    xf = x.astype(jnp.float32)
    y = xf * lax.rsqrt(jnp.mean(xf * xf, axis=-1, keepdims=True) + EPS)
    return (y * g.astype(jnp.float32)).astype(x.dtype)


def rope_tables(positions, head_dim):
    rot = head_dim // ROPE_FRACTION
    inv = ROPE_THETA ** (-jnp.arange(0, rot, 2, dtype=jnp.float32) / rot)
    ang = positions.astype(jnp.float32)[..., None] * inv
    return jnp.cos(ang)[:, :, None, :], jnp.sin(ang)[:, :, None, :]


def apply_rope(x, cos, sin):
    half = cos.shape[-1]
    x1 = x[..., :half].astype(jnp.float32)
    x2 = x[..., half:2 * half].astype(jnp.float32)
    rot = jnp.concatenate([x1 * cos - x2 * sin, x2 * cos + x1 * sin], axis=-1).astype(x.dtype)
    return jnp.concatenate([rot, x[..., 2 * half:]], axis=-1)


def diff_attention(q, k, v, lam):
    bsz, s_len, h2, d = q.shape
    n_h = h2 // 2
    nb = s_len // Q_BLOCK
    scale = 1.0 / math.sqrt(d)
    qb = q.reshape(bsz, nb, Q_BLOCK, h2, d).transpose(1, 0, 2, 3, 4)
    starts = jnp.arange(nb) * Q_BLOCK
    kpos = jnp.arange(s_len)

    def one_block(args):
        qblk, st = args
        s = jnp.einsum('bqhd,bkhd->bhqk', qblk, k, preferred_element_type=jnp.float32) * scale
        mask = kpos[None, :] <= (st + jnp.arange(Q_BLOCK))[:, None]
        p = jax.nn.softmax(jnp.where(mask, s, -jnp.inf), axis=-1)
        p = p.reshape(bsz, n_h, 2, Q_BLOCK, s_len)
        a = p[:, :, 0] - lam * p[:, :, 1]
        return jnp.einsum('bhqk,bkhe->bqhe', a.astype(v.dtype), v)

    out = lax.map(one_block, (qb, starts))
    return out.transpose(1, 0, 2, 3, 4).reshape(bsz, s_len, n_h, 2 * d)


def dilated_window_attention(q, k, v, window, dilation):
    bsz, s_len, n_h, hd = q.shape
    blk = window // dilation
    unit = blk * dilation
    s_pad = -(-s_len // unit) * unit
    nb = s_pad // unit
    scale = 1.0 / math.sqrt(hd)

    def to_blocks(t):
        t = jnp.pad(t, ((0, 0), (0, s_pad - s_len), (0, 0), (0, 0)))
        return t.reshape(bsz, nb, blk, dilation, n_h, hd)

    qb, kb, vb = to_blocks(q), to_blocks(k), to_blocks(v)
    prev = lambda t: jnp.concatenate([jnp.zeros_like(t[:, :1]), t[:, :-1]], axis=1)
    kk = jnp.concatenate([prev(kb), kb], axis=2)
    vv = jnp.concatenate([prev(vb), vb], axis=2)
    s = jnp.einsum('bnqrhd,bnkrhd->bnrhqk', qb, kk, preferred_element_type=jnp.float32) * scale
    i = jnp.arange(blk)[:, None] + blk
    j = jnp.arange(2 * blk)[None, :]
    rel = i - j
    band = (rel >= 0) & (rel <= blk)
    has_prev = jnp.arange(nb)[:, None, None] > 0
    valid = band[None] & (has_prev | (j >= blk)[None])
    s = jnp.where(valid[None, :, None, None], s, -jnp.inf)
    m = jnp.max(s, axis=-1, keepdims=True)
    p = jnp.exp(s - m)
    den = jnp.sum(p, axis=-1, keepdims=True)
    o = jnp.einsum('bnrhqk,bnkrhd->bnqrhd', (p / den).astype(v.dtype), vv)
    lse = (m + jnp.log(den))[..., 0]
    o = o.reshape(bsz, s_pad, n_h, hd)[:, :s_len]
    lse = lse.transpose(0, 1, 4, 2, 3).reshape(bsz, s_pad, n_h)[:, :s_len]
    return o, lse


def _complex_scan_combine(e1, e2):
    a1r, a1i, b1r, b1i = e1
    a2r, a2i, b2r, b2i = e2
    return (a2r * a1r - a2i * a1i,
            a2r * a1i + a2i * a1r,
            a2r * b1r - a2i * b1i + b2r,
            a2r * b1i + a2i * b1r + b2i)


def s5_branch(u, a_re, a_im, log_dt, b_re, b_im, c_re, c_im, d_skip, w_glu, b_glu):
    bsz, s_len, _ = u.shape
    f32 = jnp.float32
    uf = u.astype(f32)
    ug = uf.reshape(bsz, s_len, N_GROUPS_C, SSM_GROUP)
    a_re, a_im = a_re.astype(f32), a_im.astype(f32)
    dt = jnp.exp(log_dt.astype(f32))[:, None]
    mag = jnp.exp(a_re * dt)
    lb_re, lb_im = mag * jnp.cos(a_im * dt), mag * jnp.sin(a_im * dt)
    n_re, n_im = lb_re - 1.0, lb_im
    den = a_re * a_re + a_im * a_im
    f_re = (n_re * a_re + n_im * a_im) / den
    f_im = (n_im * a_re - n_re * a_im) / den
    b_re, b_im = b_re.astype(f32), b_im.astype(f32)
    bb_re = f_re[..., None] * b_re - f_im[..., None] * b_im
    bb_im = f_re[..., None] * b_im + f_im[..., None] * b_re
    bu_re = jnp.einsum('gpc,bsgc->bsgp', bb_re, ug)
    bu_im = jnp.einsum('gpc,bsgc->bsgp', bb_im, ug)
    la_re = jnp.broadcast_to(lb_re, bu_re.shape)
    la_im = jnp.broadcast_to(lb_im, bu_re.shape)
    _, _, x_re, x_im = lax.associative_scan(_complex_scan_combine, (la_re, la_im, bu_re, bu_im), axis=1)
    y = (jnp.einsum('gcp,bsgp->bsgc', c_re.astype(f32), x_re)
         - jnp.einsum('gcp,bsgp->bsgc', c_im.astype(f32), x_im))
    y = y.reshape(bsz, s_len, WIDTH_C) + d_skip.astype(f32) * uf
    z = jax.nn.gelu(y).astype(u.dtype)
    return z * jax.nn.sigmoid(z @ w_glu + b_glu)


def conv_ffn(h, w_up, conv_w, conv_b, w_down):
    a, b = jnp.split(h @ w_up, 2, axis=-1)
    s_len = a.shape[1]
    ap = jnp.pad(a, ((0, 0), (CONV_WIDTH - 1, 0), (0, 0)))
    a = conv_b + sum(conv_w[j] * ap[:, j:j + s_len] for j in range(CONV_WIDTH))
    return (jax.nn.silu(a) * b) @ w_down


def setup_inputs(seed: int = 0) -> dict:
    key = jax.random.key(seed)
    ks = iter(jax.random.split(key, 40))
    f32 = jnp.float32
    L = DEPTH
    nrm = lambda shape, scale: scale * jax.random.normal(next(ks), shape, f32)
    gain = lambda shape: 1.0 + 0.02 * jax.random.normal(next(ks), shape, f32)
    state_idx = jnp.arange(SSM_STATE, dtype=f32)
    return {
        "x": jax.random.normal(next(ks), (BATCH, SEQ, D_MODEL), f32),
        "positions": jnp.broadcast_to(jnp.arange(SEQ, dtype=jnp.int32), (BATCH, SEQ)),
        "attn_norm_g": gain((L, D_MODEL)),
        "w_in": nrm((L, D_MODEL, IN_COLS), D_MODEL ** -0.5),
        "b_gate": nrm((L, N_BRANCHES * D_MODEL), 0.01),
        "qn_a": gain((L, HD_A)),
        "kn_a": gain((L, HD_A)),
        "lam_q1": nrm((L, HD_A), 0.1),
        "lam_k1": nrm((L, HD_A), 0.1),
        "lam_q2": nrm((L, HD_A), 0.1),
        "lam_k2": nrm((L, HD_A), 0.1),
        "subln_g": gain((L, 2 * HD_A)),
        "w_br_a": nrm((L, WIDTH_A, D_MODEL), WIDTH_A ** -0.5),
        "qn_b": gain((L, HD_B)),
        "kn_b": gain((L, HD_B)),
        "w_br_b": nrm((L, WIDTH_B, D_MODEL), WIDTH_B ** -0.5),
        "ssm_a_re": -0.5 + nrm((L, N_GROUPS_C, SSM_STATE), 0.01),
        "ssm_a_im": jnp.pi * state_idx + nrm((L, N_GROUPS_C, SSM_STATE), 0.01),
        "ssm_log_dt": jax.random.uniform(next(ks), (L, N_GROUPS_C), f32, math.log(1e-3), math.log(1e-1)),
        "ssm_b_re": nrm((L, N_GROUPS_C, SSM_STATE, SSM_GROUP), (2 * SSM_GROUP) ** -0.5),
        "ssm_b_im": nrm((L, N_GROUPS_C, SSM_STATE, SSM_GROUP), (2 * SSM_GROUP) ** -0.5),
        "ssm_c_re": nrm((L, N_GROUPS_C, SSM_GROUP, SSM_STATE), (2 * SSM_STATE) ** -0.5),
        "ssm_c_im": nrm((L, N_GROUPS_C, SSM_GROUP, SSM_STATE), (2 * SSM_STATE) ** -0.5),
        "ssm_d": nrm((L, WIDTH_C), 1.0),
        "w_glu": nrm((L, WIDTH_C, WIDTH_C), WIDTH_C ** -0.5),
        "b_glu": nrm((L, WIDTH_C), 0.01),
        "w_br_c": nrm((L, WIDTH_C, D_MODEL), WIDTH_C ** -0.5),
        "w_out": nrm((L, D_MODEL, D_MODEL), D_MODEL ** -0.5),
        "ffn_norm_g": gain((L, D_MODEL)),
        "w_up": nrm((L, D_MODEL, 2 * D_FF), D_MODEL ** -0.5),
        "conv_w": nrm((L, CONV_WIDTH, D_FF), CONV_WIDTH ** -0.5),
        "conv_b": nrm((L, D_FF), 0.01),
        "w_down": nrm((L, D_FF, D_MODEL), D_FF ** -0.5),
    }


def reference(x, positions, attn_norm_g, w_in, b_gate, qn_a, kn_a, lam_q1, lam_k1, lam_q2, lam_k2,
              subln_g, w_br_a, qn_b, kn_b, w_br_b, ssm_a_re, ssm_a_im, ssm_log_dt, ssm_b_re, ssm_b_im,
              ssm_c_re, ssm_c_im, ssm_d, w_glu, b_glu, w_br_c, w_out, ffn_norm_g, w_up, conv_w, conv_b,
              w_down):
    bsz, s_len, _ = x.shape
    split_idx = np.cumsum(IN_SIZES)[:-1].tolist()
    cos_a, sin_a = rope_tables(positions, HD_A)
    cos_b, sin_b = rope_tables(positions, HD_B)
    for l in range(DEPTH):
        h = rms_norm(x, attn_norm_g[l])
        proj = h @ w_in[l]
        a_q, a_k, a_v, b_q, b_k, b_v, c_u, gates = jnp.split(proj, split_idx, axis=-1)

        qa = apply_rope(rms_norm(a_q.reshape(bsz, s_len, 2 * N_HEADS_A, HD_A), qn_a[l]), cos_a, sin_a)
        ka = apply_rope(rms_norm(a_k.reshape(bsz, s_len, 2 * N_HEADS_A, HD_A), kn_a[l]), cos_a, sin_a)
        va = a_v.reshape(bsz, s_len, N_HEADS_A, 2 * HD_A)
        lam_init = 0.8 - 0.6 * math.exp(-0.3 * l)
        lam = (jnp.exp(jnp.sum(lam_q1[l].astype(jnp.float32) * lam_k1[l].astype(jnp.float32)))
               - jnp.exp(jnp.sum(lam_q2[l].astype(jnp.float32) * lam_k2[l].astype(jnp.float32)))
               + lam_init)
        oa = diff_attention(qa, ka, va, lam)
        oa = (rms_norm(oa, subln_g[l]) * (1.0 - lam_init)).reshape(bsz, s_len, WIDTH_A)

        qb = apply_rope(rms_norm(b_q.reshape(bsz, s_len, N_DIL * N_HEADS_B, HD_B), qn_b[l]), cos_b, sin_b)
        kb = apply_rope(rms_norm(b_k.reshape(bsz, s_len, N_DIL * N_HEADS_B, HD_B), kn_b[l]), cos_b, sin_b)
        qb = qb.reshape(bsz, s_len, N_DIL, N_HEADS_B, HD_B)
        kb = kb.reshape(bsz, s_len, N_DIL, N_HEADS_B, HD_B)
        vb = b_v.reshape(bsz, s_len, N_DIL, N_HEADS_B, HD_B)
        outs, lses = [], []
        for g, (window, dilation) in enumerate(DIL_PAIRS):
            o_g, lse_g = dilated_window_attention(qb[:, :, g], kb[:, :, g], vb[:, :, g], window, dilation)
            outs.append(o_g)
            lses.append(lse_g)
        wts = jax.nn.softmax(jnp.stack(lses, axis=0), axis=0)
        ob = jnp.einsum('gbsh,gbshd->bshd', wts, jnp.stack(outs, axis=0).astype(jnp.float32))
        ob = ob.astype(x.dtype).reshape(bsz, s_len, WIDTH_B)

        oc = s5_branch(c_u, ssm_a_re[l], ssm_a_im[l], ssm_log_dt[l], ssm_b_re[l], ssm_b_im[l],
                       ssm_c_re[l], ssm_c_im[l], ssm_d[l], w_glu[l], b_glu[l])

        g = jax.nn.sigmoid((gates + b_gate[l]).reshape(bsz, s_len, N_BRANCHES, D_MODEL))
        merged = (g[:, :, 0] * (oa @ w_br_a[l]) + g[:, :, 1] * (ob @ w_br_b[l])
                  + g[:, :, 2] * (oc @ w_br_c[l]))
        x = x + merged @ w_out[l]

        x = x + conv_ffn(rms_norm(x, ffn_norm_g[l]), w_up[l], conv_w[l], conv_b[l], w_down[l])
    return x
```

```python
import math
from contextlib import ExitStack
import numpy as np
import ml_dtypes
import concourse.bass as bass
import concourse.mybir as mybir
from concourse.bass_utils import run_bass_kernel_spmd

F32 = mybir.dt.float32
BF16 = mybir.dt.bfloat16
I32 = mybir.dt.int32
AF = mybir.ActivationFunctionType
ALU = mybir.AluOpType
NPBF = ml_dtypes.bfloat16

D = 1024
SEQ = 8192
NB = 2
DEPTH = 4
TOK = 2048
IN_COLS = 9728
D_FF = 2816
EPS = 1e-6
PI = math.pi


class Prog:
    ENGS = ["tensor", "vector", "scalar", "gpsimd", "sync"]
    DMAQ = ["sync", "gpsimd", "scalar", "tensor"]

    def __init__(self, nc, es, n_dma_sems=10):
        self.nc = nc
        self.ops = {e: [] for e in self.ENGS}
        self.cnt = {e: 0 for e in self.ENGS}
        self.sem = {e: es.enter_context(nc.semaphore("s_" + e)) for e in ["tensor", "vector", "scalar", "gpsimd"]}
        self.dma_sems = {q: [es.enter_context(nc.semaphore(f"d_{q}{i}")) for i in range(n_dma_sems)]
                         for q in ["sync", "gpsimd"]}
        self.dma_i = {q: 0 for q in self.dma_sems}
        self.waited = {e: {} for e in self.ENGS}
        self.last_w = {}
        self.readers = {}
        self.out_tokens = []

    def _need(self, eng, tok, waits):
        sem, val, src = tok
        if src == eng and eng == "tensor":
            return
        key = id(sem)
        if self.waited[eng].get(key, 0) >= val:
            return
        self.waited[eng][key] = val
        waits.append((sem, val))

    def _deps(self, eng, reads, writes, waits):
        for k in list(reads) + list(writes):
            t = self.last_w.get(k)
            if t is not None:
                self._need(eng, t, waits)
        for k in writes:
            for t in self.readers.get(k, ()):
                self._need(eng, t, waits)

    def _commit(self, tok, reads, writes):
        for k in writes:
            self.last_w[k] = tok
            self.readers[k] = []
        for k in reads:
            self.readers.setdefault(k, []).append(tok)

    def op(self, eng, fn, reads=(), writes=()):
        waits = []
        self._deps(eng, reads, writes, waits)
        self.cnt[eng] += 1
        tok = (self.sem[eng], self.cnt[eng], eng)
        self.ops[eng].append((waits, fn, (self.sem[eng], 1)))
        self._commit(tok, reads, writes)
        return tok

    def dma(self, q, fn, reads=(), writes=(), is_output=False):
        waits = []
        self._deps(q, reads, writes, waits)
        i = self.dma_i[q]
        self.dma_i[q] += 1
        sems = self.dma_sems[q]
        sem = sems[i % len(sems)]
        prev = 16 * (i // len(sems))
        if prev > 0 and self.waited[q].get(id(sem), 0) < prev:
            self.waited[q][id(sem)] = prev
            waits.append((sem, prev))
        tok = (sem, prev + 16, "dma")
        self.ops[q].append((waits, fn, (sem, 16)))
        self._commit(tok, reads, writes)
        if is_output:
            self.out_tokens.append(tok)
        return tok

    def finish(self):
        waits = []
        for tok in self.out_tokens:
            self._need("sync", tok, waits)
        if waits:
            self.ops["sync"].append((waits, None, None))
        nc = self.nc
        with nc.Block() as block:
            def run(engname):
                def body(e):
                    for waits, fn, inc in self.ops[engname]:
                        for sem, val in waits:
                            e.wait_ge(sem, val)
                        if fn is not None:
                            fn(e).then_inc(inc[0], inc[1])
                return body
            block.tensor(run("tensor"))
            block.vector(run("vector"))
            block.scalar(run("scalar"))
            block.gpsimd(run("gpsimd"))
            block.sync(run("sync"))


class Ctx:
    def __init__(self, name):
        self.nc = bass.Bass("TRN2", target_bir_lowering=False)
        self.es = ExitStack()
        self.P = Prog(self.nc, self.es)
        self._n = 0

    def din(self, name, shape, dt):
        return self.nc.dram_tensor(name, list(shape), dt, kind="ExternalInput").ap()

    def dout(self, name, shape, dt):
        return self.nc.dram_tensor(name, list(shape), dt, kind="ExternalOutput").ap()

    def sb(self, name, shape, dt):
        return self.es.enter_context(self.nc.sbuf_tensor("sb_" + name, list(shape), dt))

    def ps(self, name, shape, dt=F32):
        return self.es.enter_context(self.nc.psum_tensor("ps_" + name, list(shape), dt))

    def close(self):
        self.P.finish()
        self.es.close()
        return self.nc


def range_reduce_sin(P, eng_v, out_sin, ang, kf, ki, m, key, shift=0.0):
    C1 = 6.28125
    C2 = 2 * PI - C1
    V = "vector"
    src = ang
    if shift != 0.0:
        P.op(V, lambda e: e.tensor_scalar(out=m, in0=ang, scalar1=shift, scalar2=None, op0=ALU.add), reads=[key + "ang"], writes=[key + "m"])
        src = m
        srck = key + "m"
    else:
        srck = key + "ang"
    P.op(V, lambda e: e.tensor_scalar(out=kf, in0=src, scalar1=1.0 / (2 * PI), scalar2=None, op0=ALU.mult), reads=[srck], writes=[key + "kf"])
    P.op(V, lambda e: e.tensor_copy(out=ki, in_=kf), reads=[key + "kf"], writes=[key + "ki"])
    P.op(V, lambda e: e.tensor_copy(out=kf, in_=ki), reads=[key + "ki"], writes=[key + "kf"])
    P.op(V, lambda e: e.scalar_tensor_tensor(out=out_sin, in0=kf, scalar=-C1, in1=src, op0=ALU.mult, op1=ALU.add), reads=[key + "kf", srck], writes=[key + "r"])
    P.op(V, lambda e: e.scalar_tensor_tensor(out=out_sin, in0=kf, scalar=-C2, in1=out_sin, op0=ALU.mult, op1=ALU.add), reads=[key + "kf", key + "r"], writes=[key + "r"])
    P.op(V, lambda e: e.tensor_single_scalar(out=kf, in_=out_sin, scalar=PI, op=ALU.is_gt), reads=[key + "r"], writes=[key + "kf"])
    P.op(V, lambda e: e.scalar_tensor_tensor(out=out_sin, in0=kf, scalar=-2 * PI, in1=out_sin, op0=ALU.mult, op1=ALU.add), reads=[key + "kf", key + "r"], writes=[key + "r"])
    P.op(V, lambda e: e.tensor_single_scalar(out=kf, in_=out_sin, scalar=-PI, op=ALU.is_lt), reads=[key + "r"], writes=[key + "kf"])
    P.op(V, lambda e: e.scalar_tensor_tensor(out=out_sin, in0=kf, scalar=2 * PI, in1=out_sin, op0=ALU.mult, op1=ALU.add), reads=[key + "kf", key + "r"], writes=[key + "r"])
    P.op(V, lambda e: e.tensor_scalar(out=out_sin, in0=out_sin, scalar1=PI, scalar2=-PI, op0=ALU.min, op1=ALU.max), reads=[key + "r"], writes=[key + "r"])
    P.op("scalar", lambda e: e.activation(out=out_sin, in_=out_sin, func=AF.Sin), reads=[key + "r"], writes=[key + "r"])


A_CHUNKS = ([("qa", 0), ("ka", 0), ("va", 0)] + [("qb", i) for i in range(3)] + [("kb", i) for i in range(3)]
            + [("vb", i) for i in range(3)] + [("cu", 0)] + [("g", i) for i in range(6)])


def build_phaseA():
    C = Ctx("A")
    nc, P = C.nc, C.P
    xT = C.din("xT", [D, TOK], F32)
    w_in = C.din("w_in", [D, IN_COLS], F32)
    g_attn = C.din("g_attn", [128, 8], F32)
    b_gate = C.din("b_gate", [128, 24], F32)
    gains = C.din("gains", [128, 4], F32)
    pos = C.din("pos", [1, TOK], I32)
    invf = C.din("invf", [128, 2], F32)
    mats = C.din("mats", [4, 128, 128], F32)
    QK = C.dout("QK", [32, 128, TOK], BF16)
    Vo = C.dout("V", [TOK, 2048], BF16)
    CU = C.dout("CU", [4, 128, TOK], BF16)
    G = C.dout("G", [24, 128, TOK], BF16)

    xt = [C.sb(f"xt{i}", [128, 8, 512], F32) for i in range(2)]
    sq8 = C.sb("sq8", [128, 8, 512], BF16)
    hT = C.sb("hT", [128, 8, TOK], BF16)
    tabs = {n: C.sb("tab_" + n, [128, TOK], BF16) for n in ["cosA", "sinA", "cosB", "sinB"]}
    wf = [C.sb(f"wf{i}", [128, 8, 512], F32) for i in range(2)]
    wb = [C.sb(f"wb{i}", [128, 8, 512], BF16) for i in range(2)]
    stage = [C.sb(f"stage{i}", [128, TOK], BF16) for i in range(3)]
    vst = [C.sb(f"vst{i}", [128, 512], BF16) for i in range(3)]
    sq = [C.sb(f"sq{i}", [128, 512], BF16) for i in range(2)]
    qg = [C.sb(f"qg{i}", [128, 512], BF16) for i in range(2)]
    rs = [C.sb(f"rs{i}", [128, 512], F32) for i in range(2)]
    t1 = [C.sb(f"t1{i}", [128, 512], F32) for i in range(2)]
    t2 = [C.sb(f"t2{i}", [128, 512], F32) for i in range(2)]
    gat = C.sb("gat", [128, 8], F32)
    bgt = C.sb("bgt", [128, 24], F32)
    gn = C.sb("gn", [128, 4], F32)
    ivf = C.sb("ivf", [128, 2], F32)
    matf = C.sb("matf", [128, 4, 128], F32)
    matb = C.sb("matb", [128, 4, 128], BF16)
    posi = C.sb("posi", [128, 512], I32)
    posf = C.sb("posf", [128, 512], F32)
    ang = C.sb("ang", [128, 512], F32)
    kf = C.sb("kf", [128, 512], F32)
    ki = C.sb("ki", [128, 512], I32)
    mm = C.sb("mm", [128, 512], F32)
    rr = C.sb("rr", [128, 512], F32)
    acc = [C.ps(f"acc{i}", [128, 512]) for i in range(3)]
    pss = [C.ps(f"pss{i}", [128, 512]) for i in range(2)]
    ppq = [C.ps(f"ppq{i}", [128, 512]) for i in range(2)]

    S = "sync"
    P.dma(S, lambda e: e.dma_start(out=gat[:], in_=g_attn), writes=["gat"])
    P.dma(S, lambda e: e.dma_start(out=bgt[:], in_=b_gate), writes=["bgt"])
    P.dma(S, lambda e: e.dma_start(out=gn[:], in_=gains), writes=["gn"])
    P.dma(S, lambda e: e.dma_start(out=ivf[:], in_=invf), writes=["ivf"])
    P.dma(S, lambda e: e.dma_start(out=matf[:], in_=mats.rearrange("m p n -> p m n")), writes=["matf"])
    P.op("vector", lambda e: e.tensor_copy(out=matb[:], in_=matf[:]), reads=["matf"], writes=["matb"])
    PmT = {"a": matb[:, 0, :], "b": matb[:, 1, :]}
    ones = {"a": matb[:, 2, :], "b": matb[:, 3, :]}

    for T in range(4):
        xb = xt[T % 2]
        xk = f"xt{T % 2}"
        P.dma(S, lambda e, xb=xb, T=T: e.dma_start(out=xb[:], in_=xT.rearrange("(k p) t -> p k t", p=128)[:, :, T * 512:(T + 1) * 512]), writes=[xk])
        P.op("scalar", lambda e, xb=xb: e.activation(out=sq8[:].rearrange("p k t -> p (k t)"), in_=xb[:].rearrange("p k t -> p (k t)"), func=AF.Square), reads=[xk], writes=["sq8"])
        pa = pss[T % 2]
        pk = f"pss{T % 2}"
        for k in range(8):
            P.op("tensor", lambda e, k=k, pa=pa: e.matmul(out=pa[:], lhsT=ones["b"], rhs=sq8[:, k, :], start=(k == 0), stop=(k == 7)), reads=["sq8", "matb"], writes=[pk])
        r = rs[T % 2]
        rk = f"rs{T % 2}"
        P.op("scalar", lambda e, r=r, pa=pa: e.activation(out=r[:], in_=pa[:], func=AF.Sqrt, scale=1.0 / D, bias=EPS), reads=[pk], writes=[rk])
        P.op("vector", lambda e, r=r: e.reciprocal(out=r[:], in_=r[:]), reads=[rk], writes=[rk])
        for k in range(8):
            eng = "vector"
            P.op(eng, lambda e, k=k, xb=xb, r=r, T=T: e.scalar_tensor_tensor(out=hT[:, k, T * 512:(T + 1) * 512], in0=xb[:, k, :], scalar=gat[:, k:k + 1], in1=r[:], op0=ALU.mult, op1=ALU.mult),
                 reads=[xk, rk, "gat"], writes=[("hT", T)])

    for T in range(4):
        P.dma(S, lambda e, T=T: e.dma_start(out=posi[:], in_=pos[:, T * 512:(T + 1) * 512].broadcast_to([128, 512])), writes=["posi"])
        P.op("vector", lambda e: e.tensor_copy(out=posf[:], in_=posi[:]), reads=["posi"], writes=["posf"])
        for j, ab in enumerate("AB"):
            P.op("vector", lambda e, j=j: e.tensor_scalar(out=ang[:], in0=posf[:], scalar1=ivf[:, j:j + 1], scalar2=None, op0=ALU.mult), reads=["posf", "ivf"], writes=["rr_ang"])
            for nm, shift in (("sin", 0.0), ("cos", PI / 2)):
                range_reduce_sin(P, "vector", rr[:], ang[:], kf[:], ki[:], mm[:], "rr_", shift=shift)
                tb = tabs[nm + ab]
                P.op("gpsimd", lambda e, tb=tb, T=T: e.tensor_copy(out=tb[:, T * 512:(T + 1) * 512], in_=rr[:]), reads=["rr_r"], writes=[("tab", nm + ab, T)])

    def load_w(c):
        b = c % 2
        P.dma(S, lambda e, b=b, c=c: e.dma_start(out=wf[b][:], in_=w_in.rearrange("(k p) n -> p k n", p=128)[:, :, c * 512:(c + 1) * 512]), writes=[f"wf{b}"])
        P.op("gpsimd", lambda e, b=b: e.tensor_copy(out=wb[b][:].rearrange("p k n -> p (k n)"), in_=wf[b][:].rearrange("p k n -> p (k n)")), reads=[f"wf{b}"], writes=[f"wb{b}"])

    load_w(0)
    acc_i = 0
    aux_i = 0
    st_i = 0
    vst_i = 0
    qk_rg = 0
    g_rg = 0
    for c, (kind, idx) in enumerate(A_CHUNKS):
        if c + 1 < len(A_CHUNKS):
            load_w(c + 1)
        b = c % 2
        wbk = f"wb{b}"
        if kind in ("va", "vb"):
            col0 = 0 if kind == "va" else 512 + idx * 512
            for tt in range(16):
                pa = acc[acc_i % 3]
                pk = f"acc{acc_i % 3}"
                acc_i += 1
                for k in range(8):
                    P.op("tensor", lambda e, k=k, pa=pa, tt=tt, b=b: e.matmul(out=pa[:], lhsT=hT[:, k, tt * 128:(tt + 1) * 128], rhs=wb[b][:, k, :], start=(k == 0), stop=(k == 7)),
                         reads=[("hT", tt // 4), wbk], writes=[pk])
                vs = vst[vst_i % 3]
                vk = f"vst{vst_i % 3}"
                vst_i += 1
                if tt % 2 == 0:
                    P.op("vector", lambda e, vs=vs, pa=pa: e.tensor_copy(out=vs[:], in_=pa[:]), reads=[pk], writes=[vk])
                else:
                    P.op("scalar", lambda e, vs=vs, pa=pa: e.activation(out=vs[:], in_=pa[:], func=AF.Copy), reads=[pk], writes=[vk])
                P.dma(S, lambda e, vs=vs, tt=tt, col0=col0: e.dma_start(out=Vo[tt * 128:(tt + 1) * 128, col0:col0 + 512], in_=vs[:]), reads=[vk], is_output=True)
            continue
        for rg in range(4):
            stg = stage[st_i % 3]
            sk = f"stage{st_i % 3}"
            st_i += 1
            for T in range(4):
                pa = acc[acc_i % 3]
                pk = f"acc{acc_i % 3}"
                acc_i += 1
                for k in range(8):
                    P.op("tensor", lambda e, k=k, pa=pa, T=T, b=b, rg=rg: e.matmul(out=pa[:], lhsT=wb[b][:, k, rg * 128:(rg + 1) * 128], rhs=hT[:, k, T * 512:(T + 1) * 512], start=(k == 0), stop=(k == 7)),
                         reads=[("hT", T), wbk], writes=[pk])
                osl = stg[:, T * 512:(T + 1) * 512]
                if kind == "cu":
                    P.op("vector", lambda e, pa=pa, osl=osl: e.tensor_copy(out=osl, in_=pa[:]), reads=[pk], writes=[(sk, T)])
                elif kind == "g":
                    col = idx * 4 + rg
                    P.op("scalar", lambda e, pa=pa, osl=osl, col=col: e.activation(out=osl, in_=pa[:], func=AF.Sigmoid, bias=bgt[:, col:col + 1]), reads=[pk, "bgt"], writes=[(sk, T)])
                else:
                    ab = kind[1]
                    gcol = {"qa": 0, "ka": 1, "qb": 2, "kb": 3}[kind]
                    dd = 64.0 if ab == "a" else 128.0
                    a = aux_i % 2
                    aux_i += 1
                    P.op("scalar", lambda e, pa=pa, a=a: e.activation(out=sq[a][:], in_=pa[:], func=AF.Square), reads=[pk], writes=[f"sq{a}"])
                    P.op("scalar", lambda e, pa=pa, a=a, gcol=gcol: e.activation(out=qg[a][:], in_=pa[:], func=AF.Copy, scale=gn[:, gcol:gcol + 1]), reads=[pk, "gn"], writes=[f"qg{a}"])
                    P.op("tensor", lambda e, a=a, ab=ab: e.matmul(out=pss[a][:], lhsT=ones[ab], rhs=sq[a][:], start=True, stop=True), reads=[f"sq{a}", "matb"], writes=[f"pss{a}"])
                    P.op("tensor", lambda e, a=a, ab=ab: e.matmul(out=ppq[a][:], lhsT=PmT[ab], rhs=qg[a][:], start=True, stop=True), reads=[f"qg{a}", "matb"], writes=[f"ppq{a}"])
                    P.op("scalar", lambda e, a=a, dd=dd: e.activation(out=rs[a][:], in_=pss[a][:], func=AF.Sqrt, scale=1.0 / dd, bias=EPS), reads=[f"pss{a}"], writes=[f"rs{a}"])
                    P.op("vector", lambda e, a=a: e.reciprocal(out=rs[a][:], in_=rs[a][:]), reads=[f"rs{a}"], writes=[f"rs{a}"])
                    ct = tabs["cos" + ab.upper()]
                    sn = tabs["sin" + ab.upper()]
                    P.op("vector", lambda e, a=a, ct=ct, T=T: e.tensor_tensor(out=t1[a][:], in0=qg[a][:], in1=ct[:, T * 512:(T + 1) * 512], op=ALU.mult),
                         reads=[f"qg{a}", ("tab", "cos" + ab.upper(), T)], writes=[f"t1{a}"])
                    P.op("vector", lambda e, a=a, sn=sn, T=T: e.tensor_tensor(out=t2[a][:], in0=ppq[a][:], in1=sn[:, T * 512:(T + 1) * 512], op=ALU.mult),
                         reads=[f"ppq{a}", ("tab", "sin" + ab.upper(), T)], writes=[f"t2{a}"])
                    P.op("gpsimd", lambda e, a=a: e.tensor_tensor(out=t1[a][:], in0=t1[a][:], in1=t2[a][:], op=ALU.add), reads=[f"t1{a}", f"t2{a}"], writes=[f"t1{a}"])
                    P.op("gpsimd", lambda e, a=a, osl=osl: e.tensor_tensor(out=osl, in0=t1[a][:], in1=rs[a][:], op=ALU.mult), reads=[f"t1{a}", f"rs{a}"], writes=[(sk, T)])
            if kind == "cu":
                dst = CU[rg]
            elif kind == "g":
                dst = G[idx * 4 + rg]
            else:
                base = {"qa": 0, "ka": 4, "qb": 8, "kb": 20}[kind]
                dst = QK[base + idx * 4 + rg]
            P.dma(S, lambda e, stg=stg, dst=dst: e.dma_start(out=dst, in_=stg[:]), reads=[(sk, T) for T in range(4)], is_output=True)
    return C.close()


def rope_consts():
    invf = np.zeros((128, 2), np.float32)
    inv_a = (500000.0 ** (-np.arange(0, 16, 2, dtype=np.float32) / 16)).astype(np.float32)
    inv_b = (500000.0 ** (-np.arange(0, 32, 2, dtype=np.float32) / 32)).astype(np.float32)
    PmA = np.zeros((128, 128), np.float32)
    PmB = np.zeros((128, 128), np.float32)
    for base in (0, 64):
        for i in range(8):
            invf[base + i, 0] = inv_a[i]
            invf[base + 8 + i, 0] = inv_a[i]
            PmA[base + i, base + i + 8] = -1.0
            PmA[base + 8 + i, base + i] = 1.0
    for i in range(16):
        invf[i, 1] = inv_b[i]
        invf[16 + i, 1] = inv_b[i]
        PmB[i, i + 16] = -1.0
        PmB[16 + i, i] = 1.0
    onesA = np.zeros((128, 128), np.float32)
    onesA[:64, :64] = 1.0
    onesA[64:, 64:] = 1.0
    onesB = np.ones((128, 128), np.float32)
    mats = np.stack([PmA.T.copy(), PmB.T.copy(), onesA, onesB]).astype(np.float32)
    return invf, mats


def phaseA_inputs(xT_core, pos_core, l, inp):
    invf, mats = rope_consts()
    gains = np.stack([np.tile(inp["qn_a"][l], 2), np.tile(inp["kn_a"][l], 2), inp["qn_b"][l], inp["kn_b"][l]], axis=1).astype(np.float32)
    return {
        "xT": np.ascontiguousarray(xT_core, dtype=np.float32),
        "w_in": np.ascontiguousarray(inp["w_in"][l]),
        "g_attn": np.ascontiguousarray(inp["attn_norm_g"][l].reshape(8, 128).T),
        "b_gate": np.ascontiguousarray(inp["b_gate"][l].reshape(24, 128).T),
        "gains": np.ascontiguousarray(gains),
        "pos": np.ascontiguousarray(pos_core.reshape(1, TOK).astype(np.int32)),
        "invf": invf,
        "mats": mats,
    }


HALO = 8
UOWN = 1024
U = HALO + UOWN
C_TILES = [(0, HALO), (HALO, 512), (HALO + 512, 512)]
N_RG_FF = D_FF // 128
FF_GROUPS = [list(range(i, min(i + 4, N_RG_FF))) for i in range(0, N_RG_FF, 4)]


def build_phaseC():
    C = Ctx("C")
    nc, P = C.nc, C.P
    NU = 2
    xT = C.din("xT", [NU, D, U], F32)
    Oin = C.din("O", [NU, 12, 128, U], BF16)
    Gin = C.din("G", [NU, 24, 128, U], BF16)
    hv = C.din("hv", [128, NU], F32)
    w_glu = C.din("w_glu", [512, 512], F32)
    b_glu = C.din("b_glu", [128, 4], F32)
    w_br = C.din("w_br", [3, 512, D], F32)
    w_out = C.din("w_out", [D, D], F32)
    g_ffn = C.din("g_ffn", [128, 8], F32)
    w_up = C.din("w_up", [D, 2 * D_FF], F32)
    conv_w = C.din("conv_w", [128, 3, N_RG_FF], F32)
    conv_b = C.din("conv_b", [128, N_RG_FF], F32)
    w_down = C.din("w_down", [D_FF, D], F32)
    ones_in = C.din("ones", [128, 128], F32)
    Xo = C.dout("Xo", [NU, D, UOWN], F32)

    x = C.sb("x", [128, 8, U], F32)
    O = C.sb("O", [128, 12, U], BF16)
    OC = C.sb("OC", [128, 4, U], BF16)
    mg = C.sb("mg", [128, 8, U], BF16)
    g3 = [C.sb(f"g3{i}", [128, 3, 512], BF16) for i in range(2)]
    stg = [C.sb(f"stg{i}", [128, 2048], F32) for i in range(2)]
    wA = [C.sb(f"wA{i}", [128, 2048], BF16) for i in range(2)]
    wD = [C.sb(f"wD{i}", [128, 4, D], BF16) for i in range(2)]
    a_sb = [C.sb(f"a_sb{i}", [128, U], F32) for i in range(2)]
    b_sb = [C.sb(f"b_sb{i}", [128, U], BF16) for i in range(2)]
    ac = [C.sb(f"ac{i}", [128, U], F32) for i in range(2)]
    hm = [C.sb(f"hm{i}", [128, 4, U], BF16) for i in range(2)]
    tt = [C.sb(f"tt{i}", [128, 512], F32) for i in range(2)]
    uu = [C.sb(f"uu{i}", [128, 512], F32) for i in range(2)]
    sq8 = C.sb("sq8", [128, 8, 512], BF16)
    rs = [C.sb(f"rs{i}", [128, 512], F32) for i in range(2)]
    bgl = C.sb("bgl", [128, 4], F32)
    gff = C.sb("gff", [128, 8], F32)
    cw = C.sb("cw", [128, 3, N_RG_FF], F32)
    cb = C.sb("cb", [128, N_RG_FF], F32)
    hvt = C.sb("hvt", [128, NU], F32)
    onf = C.sb("onf", [128, 128], F32)
    onb = C.sb("onb", [128, 128], BF16)
    banks = [C.ps(f"bk{i}", [128, 512]) for i in range(8)]
    bi = [0]

    def bank():
        i = bi[0] % 8
        bi[0] += 1
        return banks[i], f"bk{i}"

    S = "sync"
    for t, src, k in ((bgl, b_glu, "bgl"), (gff, g_ffn, "gff"), (cw, conv_w, "cw"), (cb, conv_b, "cb"), (hvt, hv, "hvt"), (onf, ones_in, "onf")):
        P.dma(S, lambda e, t=t, src=src: e.dma_start(out=t[:], in_=src), writes=[k])
    P.op("vector", lambda e: e.tensor_copy(out=onb[:], in_=onf[:]), reads=["onf"], writes=["onb"])
    for i in range(2):
        P.op("gpsimd", lambda e, i=i: e.memset(hm[i][:, :, 0:2], 0.0), writes=[f"hm{i}"])

    wi = [0]

    def load_w(src_ap, rows_k, ncols):
        i = wi[0] % 2
        wi[0] += 1
        n = rows_k * ncols
        sv = stg[i][:, 0:n].rearrange("p (k n) -> p k n", k=rows_k)
        P.dma(S, lambda e, sv=sv, src_ap=src_ap: e.dma_start(out=sv, in_=src_ap), writes=[f"stg{i}"])
        P.op("gpsimd", lambda e, i=i, n=n: e.tensor_copy(out=wA[i][:, 0:n], in_=stg[i][:, 0:n]), reads=[f"stg{i}"], writes=[f"wA{i}"])
        return wA[i][:, 0:n].rearrange("p (k n) -> p k n", k=rows_k), f"wA{i}"

    for u in range(NU):
        P.dma(S, lambda e, u=u: e.dma_start(out=x[:], in_=xT[u].rearrange("(k p) t -> p k t", p=128)), writes=["x"])
        P.dma(S, lambda e, u=u: e.dma_start(out=O[:], in_=Oin[u].rearrange("r p t -> p r t")), writes=["O"])
        for rg in range(4):
            wv, wk = load_w(w_glu.rearrange("(k p) n -> p k n", p=128)[:, :, rg * 128:(rg + 1) * 128], 4, 128)
            for (c0, n) in C_TILES:
                pb, pk = bank()
                for k in range(4):
                    P.op("tensor", lambda e, pb=pb, wv=wv, k=k, c0=c0, n=n: e.matmul(out=pb[:, 0:n], lhsT=wv[:, k, :], rhs=O[:, 8 + k, c0:c0 + n], start=(k == 0), stop=(k == 3)),
                         reads=[wk, "O"], writes=[pk])
                j = bi[0] % 2
                P.op("scalar", lambda e, pb=pb, j=j, n=n, rg=rg: e.activation(out=tt[j][:, 0:n], in_=pb[:, 0:n], func=AF.Sigmoid, bias=bgl[:, rg:rg + 1]), reads=[pk, "bgl"], writes=[f"tt{j}"])
                P.op("vector", lambda e, j=j, n=n, c0=c0, rg=rg: e.tensor_tensor(out=OC[:, rg, c0:c0 + n], in0=O[:, 8 + rg, c0:c0 + n], in1=tt[j][:, 0:n], op=ALU.mult), reads=["O", f"tt{j}"], writes=[("OC", rg)])
        gi = 0
        for oc in range(8):
            wv, wk = load_w(w_br.rearrange("b (k p) n -> p b k n", p=128)[:, :, :, oc * 128:(oc + 1) * 128].rearrange("p b k n -> p (b k) n"), 12, 128)
            for (c0, n) in C_TILES:
                gb = g3[gi % 2]
                gk = f"g3{gi % 2}"
                gi += 1
                P.dma(S, lambda e, gb=gb, u=u, oc=oc, c0=c0, n=n: e.dma_start(out=gb[:, :, 0:n], in_=Gin[u].rearrange("(b o) p t -> p b o t", b=3)[:, :, oc, c0:c0 + n]), writes=[gk])
                pbs = []
                for br in range(3):
                    pb, pk = bank()
                    pbs.append((pb, pk))
                    for k in range(4):
                        rhs = (O[:, br * 4 + k, c0:c0 + n] if br < 2 else OC[:, k, c0:c0 + n])
                        rk = "O" if br < 2 else ("OC", k)
                        P.op("tensor", lambda e, pb=pb, wv=wv, br=br, k=k, rhs=rhs, n=n: e.matmul(out=pb[:, 0:n], lhsT=wv[:, br * 4 + k, :], rhs=rhs, start=(k == 0), stop=(k == 3)),
                             reads=[wk, rk], writes=[pk])
                j = gi % 2
                P.op("vector", lambda e, j=j, n=n, gb=gb, pb=pbs[0][0]: e.tensor_tensor(out=tt[j][:, 0:n], in0=pb[:, 0:n], in1=gb[:, 0, 0:n], op=ALU.mult), reads=[pbs[0][1], gk], writes=[f"tt{j}"])
                P.op("vector", lambda e, j=j, n=n, gb=gb, pb=pbs[1][0]: e.tensor_tensor(out=uu[j][:, 0:n], in0=pb[:, 0:n], in1=gb[:, 1, 0:n], op=ALU.mult), reads=[pbs[1][1], gk], writes=[f"uu{j}"])
                P.op("gpsimd", lambda e, j=j, n=n: e.tensor_tensor(out=tt[j][:, 0:n], in0=tt[j][:, 0:n], in1=uu[j][:, 0:n], op=ALU.add), reads=[f"tt{j}", f"uu{j}"], writes=[f"tt{j}"])
                P.op("vector", lambda e, j=j, n=n, gb=gb, pb=pbs[2][0]: e.tensor_tensor(out=uu[j][:, 0:n], in0=pb[:, 0:n], in1=gb[:, 2, 0:n], op=ALU.mult), reads=[pbs[2][1], gk, f"uu{j}"], writes=[f"uu{j}"])
                P.op("gpsimd", lambda e, j=j, n=n, oc=oc, c0=c0: e.tensor_tensor(out=mg[:, oc, c0:c0 + n], in0=tt[j][:, 0:n], in1=uu[j][:, 0:n], op=ALU.add), reads=[f"tt{j}", f"uu{j}"], writes=[("mg", oc)])
        for oc in range(8):
            wv, wk = load_w(w_out.rearrange("(k p) n -> p k n", p=128)[:, :, oc * 128:(oc + 1) * 128], 8, 128)
            for (c0, n) in C_TILES:
                pb, pk = bank()
                for k in range(8):
                    P.op("tensor", lambda e, pb=pb, wv=wv, k=k, c0=c0, n=n: e.matmul(out=pb[:, 0:n], lhsT=wv[:, k, :], rhs=mg[:, k, c0:c0 + n], start=(k == 0), stop=(k == 7)),
                         reads=[wk, ("mg", k)], writes=[pk])
                P.op("vector", lambda e, pb=pb, oc=oc, c0=c0, n=n: e.tensor_tensor(out=x[:, oc, c0:c0 + n], in0=x[:, oc, c0:c0 + n], in1=pb[:, 0:n], op=ALU.add), reads=[pk, "x"], writes=["x"])
        for ti, (c0, n) in enumerate(C_TILES):
            P.op("scalar", lambda e, c0=c0, n=n: e.activation(out=sq8[:, :, 0:n], in_=x[:, :, c0:c0 + n], func=AF.Square), reads=["x"], writes=["sq8"])
            pb, pk = bank()
            for k in range(8):
                P.op("tensor", lambda e, pb=pb, k=k, n=n: e.matmul(out=pb[:, 0:n], lhsT=onb[:], rhs=sq8[:, k, 0:n], start=(k == 0), stop=(k == 7)), reads=["sq8", "onb"], writes=[pk])
            r = rs[ti % 2]
            rk = f"rs{ti % 2}"
            P.op("scalar", lambda e, r=r, pb=pb, n=n: e.activation(out=r[:, 0:n], in_=pb[:, 0:n], func=AF.Sqrt, scale=1.0 / D, bias=EPS), reads=[pk], writes=[rk])
            P.op("vector", lambda e, r=r, n=n: e.reciprocal(out=r[:, 0:n], in_=r[:, 0:n]), reads=[rk], writes=[rk])
            for k in range(8):
                P.op("vector", lambda e, k=k, r=r, c0=c0, n=n: e.scalar_tensor_tensor(out=mg[:, k, c0:c0 + n], in0=x[:, k, c0:c0 + n], scalar=gff[:, k:k + 1], in1=r[:, 0:n], op0=ALU.mult, op1=ALU.mult),
                     reads=["x", rk, "gff"] + [("mg", kk) for kk in range(8)], writes=[("h2", ti)])
        ri_glob = 0
        for gidx, grp in enumerate(FF_GROUPS):
            hb = hm[gidx % 2]
            hk = f"hm{gidx % 2}"
            wd = wD[gidx % 2]
            wdk = f"wD{gidx % 2}"
            for ri, r in enumerate(grp):
                i = wi[0] % 2
                wi[0] += 1
                for half in range(2):
                    pass
                P.dma(S, lambda e, i=i, r=r: e.dma_start(out=stg[i][:, 0:D], in_=w_down[r * 128:(r + 1) * 128, :]), writes=[f"stg{i}"])
                P.op("gpsimd", lambda e, i=i, wd=wd, ri=ri: e.tensor_copy(out=wd[:, ri, :], in_=stg[i][:, 0:D]), reads=[f"stg{i}"], writes=[(wdk, ri)])
                i = wi[0] % 2
                wi[0] += 1
                sv = stg[i][:, 0:2048].rearrange("p (k h n) -> p k h n", k=8, h=2)
                for h in range(2):
                    P.dma(S, lambda e, sv=sv, r=r, h=h: e.dma_start(out=sv[:, :, h, :], in_=w_up.rearrange("(k p) n -> p k n", p=128)[:, :, h * D_FF + r * 128:h * D_FF + (r + 1) * 128]), writes=[f"stg{i}"])
                P.op("gpsimd", lambda e, i=i: e.tensor_copy(out=wA[i][:, 0:2048], in_=stg[i][:, 0:2048]), reads=[f"stg{i}"], writes=[f"wA{i}"])
                wv = wA[i][:, 0:2048].rearrange("p (k h n) -> p k h n", k=8, h=2)
                wk = f"wA{i}"
                j = ri_glob % 2
                ri_glob += 1
                for ti, (c0, n) in enumerate(C_TILES):
                    pa, pak = bank()
                    pbb, pbk = bank()
                    for k in range(8):
                        P.op("tensor", lambda e, pa=pa, wv=wv, k=k, c0=c0, n=n: e.matmul(out=pa[:, 0:n], lhsT=wv[:, k, 0, :], rhs=mg[:, k, c0:c0 + n], start=(k == 0), stop=(k == 7)),
                             reads=[wk, ("h2", ti)], writes=[pak])
                    for k in range(8):
                        P.op("tensor", lambda e, pbb=pbb, wv=wv, k=k, c0=c0, n=n: e.matmul(out=pbb[:, 0:n], lhsT=wv[:, k, 1, :], rhs=mg[:, k, c0:c0 + n], start=(k == 0), stop=(k == 7)),
                             reads=[wk, ("h2", ti)], writes=[pbk])
                    if ti == 0:
                        P.op("scalar", lambda e, pa=pa, j=j, c0=c0, n=n, u=u: e.activation(out=a_sb[j][:, c0:c0 + n], in_=pa[:, 0:n], func=AF.Copy, scale=hvt[:, u:u + 1]), reads=[pak, "hvt"], writes=[f"a_sb{j}"])
                    else:
                        P.op("scalar", lambda e, pa=pa, j=j, c0=c0, n=n: e.activation(out=a_sb[j][:, c0:c0 + n], in_=pa[:, 0:n], func=AF.Copy), reads=[pak], writes=[f"a_sb{j}"])
                    P.op("vector", lambda e, pbb=pbb, j=j, c0=c0, n=n: e.tensor_copy(out=b_sb[j][:, c0:c0 + n], in_=pbb[:, 0:n]), reads=[pbk], writes=[f"b_sb{j}"])
                A_ = a_sb[j]
                AC = ac[j]
                P.op("vector", lambda e, A_=A_, AC=AC, r=r: e.tensor_scalar(out=AC[:, 2:U], in0=A_[:, 2:U], scalar1=cw[:, 2, r:r + 1], scalar2=cb[:, r:r + 1], op0=ALU.mult, op1=ALU.add), reads=[f"a_sb{j}", "cw", "cb"], writes=[f"ac{j}"])
                P.op("vector", lambda e, A_=A_, AC=AC, r=r: e.scalar_tensor_tensor(out=AC[:, 2:U], in0=A_[:, 1:U - 1], scalar=cw[:, 1, r:r + 1], in1=AC[:, 2:U], op0=ALU.mult, op1=ALU.add), reads=[f"a_sb{j}", "cw", f"ac{j}"], writes=[f"ac{j}"])
                P.op("vector", lambda e, A_=A_, AC=AC, r=r: e.scalar_tensor_tensor(out=AC[:, 2:U], in0=A_[:, 0:U - 2], scalar=cw[:, 0, r:r + 1], in1=AC[:, 2:U], op0=ALU.mult, op1=ALU.add), reads=[f"a_sb{j}", "cw", f"ac{j}"], writes=[f"ac{j}"])
                P.op("scalar", lambda e, AC=AC: e.activation(out=AC[:, 2:U], in_=AC[:, 2:U], func=AF.Silu), reads=[f"ac{j}"], writes=[f"ac{j}"])
                P.op("gpsimd", lambda e, AC=AC, hb=hb, ri=ri, j=j: e.tensor_tensor(out=hb[:, ri, 2:U], in0=AC[:, 2:U], in1=b_sb[j][:, 2:U], op=ALU.mult), reads=[f"ac{j}", f"b_sb{j}"], writes=[(hk, ri)])
            for oc in range(8):
                for (c0, n) in C_TILES[1:]:
                    pb, pk = bank()
                    for ri in range(len(grp)):
                        P.op("tensor", lambda e, pb=pb, wd=wd, hb=hb, ri=ri, oc=oc, c0=c0, n=n: e.matmul(out=pb[:, 0:n], lhsT=wd[:, ri, oc * 128:(oc + 1) * 128], rhs=hb[:, ri, c0:c0 + n], start=(ri == 0), stop=(ri == len(grp) - 1)),
                             reads=[(wdk, ri), (hk, ri)], writes=[pk])
                    P.op("vector", lambda e, pb=pb, oc=oc, c0=c0, n=n: e.tensor_tensor(out=x[:, oc, c0:c0 + n], in0=x[:, oc, c0:c0 + n], in1=pb[:, 0:n], op=ALU.add), reads=[pk, "x"], writes=["x"])
        P.dma(S, lambda e, u=u: e.dma_start(out=Xo[u].rearrange("(k p) t -> p k t", p=128), in_=x[:, :, HALO:U]), reads=["x"], is_output=True)
    return C.close()


def phaseC_weight_inputs(l, inp):
    return {
        "w_glu": np.ascontiguousarray(inp["w_glu"][l]),
        "b_glu": np.ascontiguousarray(inp["b_glu"][l].reshape(4, 128).T),
        "w_br": np.ascontiguousarray(np.stack([inp["w_br_a"][l], inp["w_br_b"][l], inp["w_br_c"][l]])),
        "w_out": np.ascontiguousarray(inp["w_out"][l]),
        "g_ffn": np.ascontiguousarray(inp["ffn_norm_g"][l].reshape(8, 128).T),
        "w_up": np.ascontiguousarray(inp["w_up"][l]),
        "conv_w": np.ascontiguousarray(inp["conv_w"][l].reshape(3, N_RG_FF, 128).transpose(2, 0, 1)),
        "conv_b": np.ascontiguousarray(inp["conv_b"][l].reshape(N_RG_FF, 128).T),
        "w_down": np.ascontiguousarray(inp["w_down"][l]),
        "ones": np.ones((128, 128), np.float32),
    }


def make_units(full_T, j):
    out = []
    for uu_ in range(2):
        t0 = j * TOK + uu_ * UOWN
        if t0 == 0:
            pad = np.zeros(full_T.shape[:-1] + (HALO,), full_T.dtype)
            out.append(np.concatenate([pad, full_T[..., 0:UOWN]], axis=-1))
        else:
            out.append(full_T[..., t0 - HALO:t0 + UOWN])
    return np.ascontiguousarray(np.stack(out))


DILS = (1, 4, 16)
DEBUG = False
SKIP_ADD = False
SSM_MS = (0, 1, 2, 3)


class PhaseB:
    def __init__(self, parts=("ssm", "a", "b")):
        C = self.C = Ctx("B")
        self.nc, self.P = C.nc, C.P
        P = self.P
        self.QA = C.din("QA", [2, 128, SEQ], BF16)
        self.VA = C.din("VA", [SEQ, 128], BF16)
        self.QB = C.din("QB", [3, 2, 128, SEQ], BF16)
        self.VB = C.din("VB", [3, SEQ, 128], BF16)
        self.CUi = C.din("CU", [128, SEQ], BF16)
        self.lamv = C.din("lamv", [1, 256], F32)
        self.lcon = C.din("lcon", [128, 2], F32)
        self.subg = C.din("subg", [1, 128], F32)
        self.cmat = C.din("cmat", [4, 128, 128], F32)
        self.OA = C.dout("OA", [128, SEQ], BF16)
        self.OB = C.dout("OB", [128, SEQ], BF16)
        self.Z = C.dout("Z", [128, SEQ], BF16)
        self.dbg = C.dout("dbg", [128, 1024], F32) if DEBUG else None
        self.R0 = C.sb("R0", [128, SEQ], F32)
        self.R1 = C.sb("R1", [128, SEQ], F32)
        self.R2 = C.sb("R2", [128, 2 * SEQ], BF16)
        self.R3 = C.sb("R3", [128, 64 * 130], BF16)
        self.banks = [C.ps(f"bk{i}", [128, 512]) for i in range(8)]
        self.ost = [C.sb(f"ost{i}", [128, 2048], BF16) for i in range(2)]
        self.PT = [C.sb(f"PT{i}", [128, 512], BF16) for i in range(4)]
        self.cmf = C.sb("cmf", [128, 4, 128], F32)
        self.cmb = C.sb("cmb", [128, 4, 128], BF16)
        S = "sync"
        P.dma(S, lambda e: e.dma_start(out=self.cmf[:], in_=self.cmat.rearrange("m p n -> p m n")), writes=["cmf"])
        P.op("vector", lambda e: e.tensor_copy(out=self.cmb[:], in_=self.cmf[:]), reads=["cmf"], writes=["cmb"])
        self.ost_i = 0
        if "ssm" in parts:
            self.ssm()
        if "a" in parts:
            self.attn_a()
        if "b" in parts:
            self.attn_b()

    def close(self):
        return self.C.close()

    def attn_a(self):
        C, P = self.C, self.P
        S = "sync"
        Q = self.R2[:, 0:SEQ]
        K = self.R2[:, SEQ:2 * SEQ]
        V = self.R3[:].rearrange("p (n e) -> p n e", e=130)
        banks = self.banks
        lv = C.sb("lv", [128, 256], F32)
        lc = C.sb("lc", [128, 2], F32)
        gB = C.sb("gB", [128, 128], F32)
        sm = C.sb("sm", [128, 8], F32)
        mhalf = C.sb("mhalf", [128, 1], F32)
        o0 = C.sb("o0", [128, 4, 128], F32)
        oa = [C.sb(f"oa{i}", [128, 128], F32) for i in range(2)]
        sqv = C.sb("sqv", [128, 128], F32)
        rec = [C.sb(f"rec{i}", [128, 4], F32) for i in range(2)]
        for c in range(4):
            sl = slice(c * 2048, (c + 1) * 2048)
            P.dma(S, lambda e, sl=sl: e.dma_start(out=Q[:, sl], in_=self.QA[0][:, sl]), writes=[("R2a", c)])
            P.dma(S, lambda e, sl=sl: e.dma_start(out=K[:, sl], in_=self.QA[1][:, sl]), writes=[("R2b", c)])
        for c in range(8):
            P.dma(S, lambda e, c=c: e.dma_start(out=V[:, c * 8:(c + 1) * 8, 0:128], in_=self.VA.rearrange("(n p) e -> p n e", p=128)[:, c * 8:(c + 1) * 8, :]), writes=[("R3", c)])
            P.op("gpsimd", lambda e, c=c: e.memset(V[:, c * 8:(c + 1) * 8, 128:129], 1.0), reads=[("R3", c)], writes=[("R3", c)])
        P.dma(S, lambda e: e.dma_start(out=lv[:], in_=self.lamv.broadcast_to([128, 256])), writes=["lv"])
        P.dma(S, lambda e: e.dma_start(out=lc[:], in_=self.lcon), writes=["lc"])
        P.dma(S, lambda e: e.dma_start(out=gB[:], in_=self.subg.broadcast_to([128, 128])), writes=["gB"])
        V_ = "vector"
        P.op(V_, lambda e: e.memset(mhalf[:], -0.5), writes=["mhalf"])
        P.op(V_, lambda e: e.tensor_tensor(out=lv[:, 0:64], in0=lv[:, 0:64], in1=lv[:, 64:128], op=ALU.mult), reads=["lv"], writes=["lv"])
        P.op(V_, lambda e: e.tensor_tensor(out=lv[:, 128:192], in0=lv[:, 128:192], in1=lv[:, 192:256], op=ALU.mult), reads=["lv"], writes=["lv"])
        P.op(V_, lambda e: e.tensor_reduce(out=sm[:, 0:1], in_=lv[:, 0:64], axis=mybir.AxisListType.X, op=ALU.add), reads=["lv"], writes=["sm"])
        P.op(V_, lambda e: e.tensor_reduce(out=sm[:, 1:2], in_=lv[:, 128:192], axis=mybir.AxisListType.X, op=ALU.add), reads=["lv"], writes=["sm"])
        P.op("scalar", lambda e: e.activation(out=sm[:, 2:4], in_=sm[:, 0:2], func=AF.Exp), reads=["sm"], writes=["sm"])
        P.op(V_, lambda e: e.tensor_tensor(out=sm[:, 4:5], in0=sm[:, 3:4], in1=sm[:, 2:3], op=ALU.subtract), reads=["sm"], writes=["sm"])
        P.op(V_, lambda e: e.tensor_tensor(out=sm[:, 5:6], in0=sm[:, 4:5], in1=lc[:, 0:1], op=ALU.subtract), reads=["sm", "lc"], writes=["sm"])
        P.op(V_, lambda e: e.tensor_scalar(out=gB[:], in0=gB[:], scalar1=lc[:, 1:2], scalar2=None, op0=ALU.mult), reads=["gB", "lc"], writes=["gB"])
        nlam = sm[:, 5:6]
        mask = self.cmb[:, 0, :]
        identf = self.cmf[:, 2, :]
        st_i = 0
        pt_i = 0
        for Qt in range(16):
            q0 = Qt * 512
            for i in range(2):
                rows = slice(64 * i, 64 * i + 64)
                nkb = 4 * (Qt + 1)
                pend = None
                for kb in range(nkb + 1):
                    if kb < nkb:
                        qlo = max(0, kb * 128 - q0)
                        sb_ = banks[4 + st_i % 3]
                        sk = f"bk{4 + st_i % 3}"
                        st_i += 1
                        pt = self.PT[pt_i % 4]
                        pk = f"PT{pt_i % 4}"
                        pt_i += 1
                        P.op("tensor", lambda e, sb_=sb_, rows=rows, kb=kb, qlo=qlo, q0=q0: e.matmul(out=sb_[:, qlo:512], lhsT=K[rows, kb * 128:(kb + 1) * 128], rhs=Q[rows, q0 + qlo:q0 + 512], start=True, stop=True),
                             reads=[("R2b", kb // 16), ("R2a", Qt // 4)], writes=[sk])
                        P.op("scalar", lambda e, sb_=sb_, pt=pt, qlo=qlo: e.activation(out=pt[:, qlo:512], in_=sb_[:, qlo:512], func=AF.Exp, scale=0.125), reads=[sk], writes=[pk])
                        if kb * 128 >= q0:
                            P.op("gpsimd", lambda e, pt=pt, qlo=qlo: e.tensor_tensor(out=pt[:, qlo:qlo + 128], in0=pt[:, qlo:qlo + 128], in1=mask, op=ALU.mult), reads=[pk, "cmb"], writes=[pk])
                        cur = (kb, qlo, pt, pk)
                    else:
                        cur = None
                    if pend is not None:
                        kb2, qlo2, pt2, pk2 = pend
                        for qs in range(qlo2 // 128, 4):
                            last = (kb2 == 4 * Qt + qs)
                            P.op("tensor", lambda e, qs=qs, pt2=pt2, kb2=kb2, last=last: e.matmul(out=banks[qs][:, 0:129], lhsT=pt2[:, qs * 128:(qs + 1) * 128], rhs=V[:, kb2, 0:129], start=(kb2 == 0), stop=last),
                                 reads=[pk2, ("R3", kb2 // 8)], writes=[f"bk{qs}"])
                    pend = cur
                rc = rec[i]
                rk = f"rec{i}"
                for qs in range(4):
                    P.op(V_, lambda e, qs=qs, rc=rc: e.reciprocal(out=rc[:, qs:qs + 1], in_=banks[qs][:, 128:129]), reads=[f"bk{qs}"], writes=[(rk, qs)])
                    if i == 0:
                        P.op(V_, lambda e, qs=qs, rc=rc: e.tensor_scalar(out=o0[:, qs, :], in0=banks[qs][:, 0:128], scalar1=rc[:, qs:qs + 1], scalar2=None, op0=ALU.mult), reads=[f"bk{qs}", (rk, qs)], writes=[("o0", qs)])
                    else:
                        ob = oa[qs % 2]
                        ok = f"oa{qs % 2}"
                        P.op(V_, lambda e, qs=qs, rc=rc: e.tensor_tensor(out=rc[:, qs:qs + 1], in0=rc[:, qs:qs + 1], in1=nlam, op=ALU.mult), reads=[(rk, qs), "sm"], writes=[(rk, qs)])
                        P.op(V_, lambda e, qs=qs, rc=rc, ob=ob: e.scalar_tensor_tensor(out=ob[:], in0=banks[qs][:, 0:128], scalar=rc[:, qs:qs + 1], in1=o0[:, qs, :], op0=ALU.mult, op1=ALU.add),
                             reads=[f"bk{qs}", (rk, qs), ("o0", qs)], writes=[ok])
                        P.op(V_, lambda e, ob=ob: e.tensor_tensor(out=sqv[:], in0=ob[:], in1=ob[:], op=ALU.mult), reads=[ok], writes=["sqv"])
                        P.op(V_, lambda e, qs=qs: e.tensor_reduce(out=sm[:, 6:7], in_=sqv[:], axis=mybir.AxisListType.X, op=ALU.add), reads=["sqv"], writes=["sm6"])
                        P.op(V_, lambda e: e.tensor_scalar(out=sm[:, 6:7], in0=sm[:, 6:7], scalar1=1.0 / 128, scalar2=EPS, op0=ALU.mult, op1=ALU.add), reads=["sm6"], writes=["sm6"])
                        P.op("gpsimd", lambda e: e.tensor_tensor(out=sm[:, 7:8], in0=sm[:, 6:7], in1=mhalf[:], op=ALU.pow), reads=["sm6", "mhalf"], writes=["sm7"])
                        P.op(V_, lambda e, ob=ob: e.scalar_tensor_tensor(out=ob[:], in0=ob[:], scalar=sm[:, 7:8], in1=gB[:], op0=ALU.mult, op1=ALU.mult), reads=[ok, "sm7", "gB"], writes=[ok])
                        P.op("tensor", lambda e, qs=qs, ob=ob: e.transpose(out=banks[7][:, qs * 128:(qs + 1) * 128], in_=ob[:], identity=identf), reads=[ok, "cmf"], writes=[("bk7", qs)])
                if i == 1:
                    os_ = self.ost[self.ost_i % 2]
                    osk = f"ost{self.ost_i % 2}"
                    P.op(V_, lambda e, os_=os_, Qt=Qt: e.tensor_copy(out=os_[:, (Qt % 4) * 512:(Qt % 4 + 1) * 512], in_=banks[7][:]), reads=[("bk7", qs) for qs in range(4)], writes=[osk])
                    if Qt % 4 == 3:
                        P.dma(S, lambda e, os_=os_, Qt=Qt: e.dma_start(out=self.OA[:, (Qt - 3) * 512:(Qt + 1) * 512], in_=os_[:]), reads=[osk], is_output=True)
                        self.ost_i += 1
        if getattr(self, "dbg", None) is not None:
            P.dma(S, lambda e: e.dma_start(out=self.dbg[:, 0:8], in_=sm[:]), reads=["sm", "sm6", "sm7"], is_output=True)
            P.dma(S, lambda e: e.dma_start(out=self.dbg[:, 8:136], in_=gB[:]), reads=["gB"], is_output=True)
            P.dma(S, lambda e: e.dma_start(out=self.dbg[:, 136:140], in_=rec[1][:]), reads=[("rec1", q_) for q_ in range(4)], is_output=True)
            P.dma(S, lambda e: e.dma_start(out=self.dbg[:, 140:268], in_=oa[1][:]), reads=["oa1"], is_output=True)
            P.dma(S, lambda e: e.dma_start(out=self.dbg[:, 268:780], in_=o0[:].rearrange("p a b -> p (a b)")), reads=[("o0", q_) for q_ in range(4)], is_output=True)


def phaseB_consts():
    k = np.arange(128)[:, None]
    q = np.arange(128)[None, :]
    return np.stack([(k <= q), (k >= q), np.eye(128, dtype=bool), np.ones((128, 128), bool)]).astype(np.float32)


def phaseB_inputs_from_ref(ref, inp, l, b, h):
    bf = NPBF
    qa = ref["qa"][b].reshape(SEQ, 4, 128)[:, h].T
    ka = ref["ka"][b].reshape(SEQ, 4, 128)[:, h].T
    qb = ref["qb"][b].reshape(SEQ, 3, 4, 128)[:, :, h]
    kb = ref["kb"][b].reshape(SEQ, 3, 4, 128)[:, :, h]
    vb = ref["vb"][b].reshape(SEQ, 3, 4, 128)[:, :, h]
    m = {
        "QA": np.stack([qa, ka]).astype(bf),
        "VA": np.ascontiguousarray(ref["va"][b][:, h * 128:(h + 1) * 128]).astype(bf),
        "QB": np.stack([np.stack([qb[:, g].T, kb[:, g].T]) for g in range(3)]).astype(bf),
        "VB": np.stack([vb[:, g] for g in range(3)]).astype(bf),
        "CU": np.ascontiguousarray(ref["cu"][b][:, h * 128:(h + 1) * 128].T).astype(bf),
    }
    m.update(phaseB_param_inputs(l, h, inp))
    return m


def phaseB_param_inputs(l, h, inp):
    lam_init = 0.8 - 0.6 * math.exp(-0.3 * l)
    lcon = np.zeros((128, 2), np.float32)
    lcon[:, 0] = lam_init
    lcon[:, 1] = 1.0 - lam_init
    m = {
        "lamv": np.concatenate([inp["lam_q1"][l], inp["lam_k1"][l], inp["lam_q2"][l], inp["lam_k2"][l]]).reshape(1, 256).astype(np.float32),
        "lcon": lcon,
        "subg": inp["subln_g"][l].reshape(1, 128).astype(np.float32),
        "cmat": phaseB_consts(),
    }
    m.update(ssm_param_inputs(l, h, inp))
    return m


def _attn_b(self):
    C, P = self.C, self.P
    S = "sync"
    V_ = "vector"
    Q = self.R2[:, 0:SEQ]
    K = self.R2[:, SEQ:2 * SEQ]
    V = self.R3[:].rearrange("p (n e) -> p n e", e=130)
    banks = self.banks
    mask2 = self.cmb[:, 0:2, :].rearrange("p m n -> p (m n)")
    ones = self.cmb[:, 3, :]
    scale_b = 1.0 / math.sqrt(128.0)
    allq = [("R2a", c) for c in range(4)]
    allk = [("R2b", c) for c in range(4)]
    st_i = 0
    pt_i = 0
    for g, dil in enumerate(DILS):
        nb = SEQ // (128 * dil)
        for c in range(4):
            sl = slice(c * 2048, (c + 1) * 2048)
            P.dma(S, lambda e, sl=sl, g=g: e.dma_start(out=Q[:, sl], in_=self.QB[g][0][:, sl]), writes=[("R2a", c)])
            P.dma(S, lambda e, sl=sl, g=g: e.dma_start(out=K[:, sl], in_=self.QB[g][1][:, sl]), writes=[("R2b", c)])
        vsrc = self.VB[g].rearrange("(n j r) e -> j r n e", j=128, r=dil)
        for res in range(dil):
            for n0 in range(0, nb, 16):
                n1 = min(nb, n0 + 16)
                b0 = res * nb + n0
                b1 = res * nb + n1
                P.dma(S, lambda e, res=res, n0=n0, n1=n1, b0=b0, b1=b1, vsrc=vsrc: e.dma_start(out=V[:, b0:b1, 0:128], in_=vsrc[:, res, n0:n1, :]),
                      writes=[("R3", c) for c in range(b0 // 8, (b1 - 1) // 8 + 1)])
        for res in range(dil):
            pts = {}
            for n in range(nb + 1):
                if n < nb:
                    ks = n * 128 * dil + res
                    nq = 256 if n + 1 < nb else 128
                    sb_ = banks[4 + st_i % 3]
                    sk = f"bk{4 + st_i % 3}"
                    st_i += 1
                    pt = self.PT[pt_i % 4]
                    pk = f"PT{pt_i % 4}"
                    pt_i += 1
                    ksl = slice(ks, ks + 127 * dil + 1, dil)
                    qsl = slice(ks, ks + (nq - 1) * dil + 1, dil)
                    P.op("tensor", lambda e, sb_=sb_, ksl=ksl, qsl=qsl, nq=nq: e.matmul(out=sb_[:, 0:nq], lhsT=K[:, ksl], rhs=Q[:, qsl], start=True, stop=True),
                         reads=allq + allk, writes=[sk])
                    P.op("scalar", lambda e, sb_=sb_, pt=pt, nq=nq: e.activation(out=pt[:, 0:nq], in_=sb_[:, 0:nq], func=AF.Exp, scale=scale_b), reads=[sk], writes=[pk])
                    P.op("gpsimd", lambda e, pt=pt, nq=nq: e.tensor_tensor(out=pt[:, 0:nq], in0=pt[:, 0:nq], in1=mask2[:, 0:nq], op=ALU.mult), reads=[pk, "cmb"], writes=[pk])
                    pts[n] = (pt, pk)
                m = n - 1
                if m >= 0:
                    pair = (m // 4) % 2
                    OT, otk = banks[2 * pair], f"bk{2 * pair}"
                    DN, dnk = banks[2 * pair + 1], f"bk{2 * pair + 1}"
                    c0 = (m % 4) * 128
                    blk = res * nb + m
                    ptc, pck = pts[m]
                    if m >= 1:
                        ptp, ppk = pts[m - 1]
                        P.op("tensor", lambda e, OT=OT, c0=c0, blk=blk, ptp=ptp: e.matmul(out=OT[:, c0:c0 + 128], lhsT=V[:, blk - 1, 0:128], rhs=ptp[:, 128:256], start=True, stop=False),
                             reads=[ppk, ("R3", (blk - 1) // 8)], writes=[otk])
                        P.op("tensor", lambda e, DN=DN, c0=c0, ptp=ptp: e.matmul(out=DN[:, c0:c0 + 128], lhsT=ones, rhs=ptp[:, 128:256], start=True, stop=False),
                             reads=[ppk, "cmb"], writes=[dnk])
                    P.op("tensor", lambda e, OT=OT, c0=c0, blk=blk, ptc=ptc, m=m: e.matmul(out=OT[:, c0:c0 + 128], lhsT=V[:, blk, 0:128], rhs=ptc[:, 0:128], start=(m == 0), stop=True),
                         reads=[pck, ("R3", blk // 8)], writes=[otk])
                    P.op("tensor", lambda e, DN=DN, c0=c0, ptc=ptc, m=m: e.matmul(out=DN[:, c0:c0 + 128], lhsT=ones, rhs=ptc[:, 0:128], start=(m == 0), stop=True),
                         reads=[pck, "cmb"], writes=[dnk])
                    if m % 4 == 3:
                        m0 = m - 3
                        ts = m0 * 128 * dil + res
                        asl = slice(ts, ts + 511 * dil + 1, dil)
                        if g == 0:
                            P.op(V_, lambda e, OT=OT, asl=asl: e.tensor_copy(out=self.R0[:, asl], in_=OT[:]), reads=[otk], writes=["R0"])
                            P.op(V_, lambda e, DN=DN, asl=asl: e.tensor_copy(out=self.R1[:, asl], in_=DN[:]), reads=[dnk], writes=["R1"])
                        else:
                            P.op(V_, lambda e, OT=OT, asl=asl: e.tensor_tensor(out=self.R0[:, asl], in0=self.R0[:, asl], in1=OT[:], op=ALU.add), reads=[otk, "R0"], writes=["R0"])
                            P.op(V_, lambda e, DN=DN, asl=asl: e.tensor_tensor(out=self.R1[:, asl], in0=self.R1[:, asl], in1=DN[:], op=ALU.add), reads=[dnk, "R1"], writes=["R1"])
    for c in range(4):
        sl = slice(c * 2048, (c + 1) * 2048)
        os_ = self.ost[self.ost_i % 2]
        osk = f"ost{self.ost_i % 2}"
        self.ost_i += 1
        P.op(V_, lambda e, sl=sl: e.reciprocal(out=self.R1[:, sl], in_=self.R1[:, sl]), reads=["R1"], writes=["R1"])
        P.op(V_, lambda e, sl=sl, os_=os_: e.tensor_tensor(out=os_[:], in0=self.R0[:, sl], in1=self.R1[:, sl], op=ALU.mult), reads=["R0", "R1"], writes=[osk])
        P.dma(S, lambda e, sl=sl, os_=os_: e.dma_start(out=self.OB[:, sl], in_=os_[:]), reads=[osk], is_output=True)


PhaseB.attn_b = _attn_b


def ssm_consts():
    ch = np.arange(128)[:, None]
    col = np.arange(128)[None, :]
    mats = []
    for m in range(4):
        mats.append(((col // 16) == (2 * m + ch // 64)).astype(np.float32))
    for m in range(4):
        mats.append(((ch // 16) == (2 * m + col // 64)).astype(np.float32))
    mats.append(np.tile(np.arange(128, dtype=np.float32)[None, :], (128, 1)))
    return np.stack(mats)


def ssm_param_inputs(l, h, inp):
    gs = slice(8 * h, 8 * h + 8)
    def rows(a):
        return np.ascontiguousarray(a.reshape(4, 2, 64).transpose(1, 2, 0).reshape(128, 4))
    ssmp = np.zeros((128, 16), np.float32)
    ssmp[:, 0:4] = rows(inp["ssm_a_re"][l, gs])
    ssmp[:, 4:8] = rows(inp["ssm_a_im"][l, gs])
    ssmp[:, 8:12] = rows(np.repeat(inp["ssm_log_dt"][l, gs][:, None], 64, axis=1))
    ssmp[:, 12] = inp["ssm_d"][l, 128 * h:128 * h + 128]
    def brows(a):
        return a.reshape(4, 2, 64, 16).transpose(1, 2, 0, 3).reshape(128, 4, 16)
    ssmb = np.ascontiguousarray(np.stack([brows(inp["ssm_b_re"][l, gs]), brows(inp["ssm_b_im"][l, gs])], axis=1))
    ssmc = np.ascontiguousarray(np.stack([inp["ssm_c_re"][l, gs].reshape(128, 64), inp["ssm_c_im"][l, gs].reshape(128, 64)], axis=1))
    return {"ssmp": ssmp, "ssmb": ssmb.astype(np.float32), "ssmc": ssmc.astype(np.float32), "ssmk": ssm_consts()}


def _ssm(self):
    C, P = self.C, self.P
    S = "sync"
    V_ = "vector"
    G_ = "gpsimd"
    A_ = "scalar"
    AX = mybir.AxisListType.X
    banks = self.banks
    ssmp_d = C.din("ssmp", [128, 16], F32)
    ssmb_d = C.din("ssmb", [128, 2, 4, 16], F32)
    ssmc_d = C.din("ssmc", [128, 2, 64], F32)
    ssmk_d = C.din("ssmk", [9, 128, 128], F32)
    WR, WI = self.R0, self.R1
    Y = self.R2[:].bitcast(F32)
    CU = self.R3[:, 0:SEQ]
    allr3 = [("R3", c) for c in range(8)]

    def ykey(Tt):
        return ("R2a", Tt // 2) if Tt < 8 else ("R2b", (Tt - 8) // 2)

    pp = C.sb("s_pp", [128, 16], F32)
    bb = C.sb("s_bb", [128, 2, 4, 16], F32)
    cc = C.sb("s_cc", [128, 2, 2, 64], F32)
    kk = C.sb("s_kk", [128, 9, 128], F32)
    sc = C.sb("s_sc", [128, 40, 4], F32)
    bbr = C.sb("s_bbr", [128, 2, 4, 16], F32)
    wtmp = C.sb("s_wtmp", [128, 128], F32)
    lB = C.sb("s_lB", [128, 2, 4, 128], BF16)
    lC = C.sb("s_lC", [128, 3, 4, 128], BF16)
    cosT = C.sb("s_cosT", [128, 4, 128], F32)
    sinT = C.sb("s_sinT", [128, 4, 128], F32)
    RP = C.sb("s_RP", [128, 4, 128], F32)
    MUL = C.sb("s_MUL", [128, 2048], F32)
    T = [C.sb(f"s_T{i}", [128, 512], F32) for i in range(8)]
    VV = [C.sb(f"s_V{i}", [128, 512], BF16) for i in range(8)]
    ki = C.sb("s_ki", [128, 512], I32)
    L2 = C.sb("s_L2", [128, 12, 64], F32)

    def scol(i):
        return sc[:, i, :]

    P.dma(S, lambda e: e.dma_start(out=pp[:], in_=ssmp_d), writes=["pp"])
    P.dma(S, lambda e: e.dma_start(out=bb[:], in_=ssmb_d), writes=["bb"])
    for dup in range(2):
        P.dma(S, lambda e, dup=dup: e.dma_start(out=cc[:, :, dup, :], in_=ssmc_d), writes=["cc"])
    P.dma(S, lambda e: e.dma_start(out=kk[:], in_=ssmk_d.rearrange("m p n -> p m n")), writes=["kk"])
    for c in range(4):
        sl = slice(c * 2048, (c + 1) * 2048)
        P.dma(S, lambda e, sl=sl: e.dma_start(out=CU[:, sl], in_=self.CUi[:, sl]), writes=[("R3", 2 * c), ("R3", 2 * c + 1)])
    iota = kk[:, 8, :]
    are, aim, ldt, dcol = pp[:, 0:4], pp[:, 4:8], pp[:, 8:12], pp[:, 12:13]

    def vop(fn, reads, writes, eng=V_):
        P.op(eng, fn, reads=reads, writes=writes)

    DT, ADT, TH, MAG, SN, CS, LRE, LIM, NRE, DEN, FRE, FIM, X1, X2, C127, S127, NS127, R128, C128, S128, NLIM = range(21)
    AK = 21
    vop(lambda e: e.activation(out=scol(DT), in_=ldt, func=AF.Exp), ["pp"], ["sc"], A_)
    vop(lambda e: e.tensor_tensor(out=scol(ADT), in0=are, in1=scol(DT), op=ALU.mult), ["pp", "sc"], ["sc"])
    vop(lambda e: e.tensor_tensor(out=scol(TH), in0=aim, in1=scol(DT), op=ALU.mult), ["pp", "sc"], ["sc"])
    vop(lambda e: e.activation(out=scol(MAG), in_=scol(ADT), func=AF.Exp), ["sc"], ["sc"], A_)
    P.op(V_, lambda e: e.tensor_copy(out=T[0][:, 0:4], in_=scol(TH)), reads=["sc"], writes=["rrs_ang"])
    range_reduce_sin(P, V_, T[1][:, 0:4], T[0][:, 0:4], T[2][:, 0:4], ki[:, 0:4], T[3][:, 0:4], "rrs_", shift=0.0)
    vop(lambda e: e.tensor_copy(out=scol(SN), in_=T[1][:, 0:4]), ["rrs_r"], ["sc"])
    range_reduce_sin(P, V_, T[1][:, 0:4], T[0][:, 0:4], T[2][:, 0:4], ki[:, 0:4], T[3][:, 0:4], "rrs_", shift=PI / 2)
    vop(lambda e: e.tensor_copy(out=scol(CS), in_=T[1][:, 0:4]), ["rrs_r"], ["sc"])
    tt_ = lambda o, a, b, op: vop(lambda e: e.tensor_tensor(out=scol(o), in0=scol(a), in1=scol(b), op=op), ["sc"], ["sc"])
    tt_(LRE, MAG, CS, ALU.mult)
    tt_(LIM, MAG, SN, ALU.mult)
    vop(lambda e: e.tensor_scalar(out=scol(NRE), in0=scol(LRE), scalar1=-1.0, scalar2=None, op0=ALU.add), ["sc"], ["sc"])
    vop(lambda e: e.tensor_scalar(out=scol(NLIM), in0=scol(LIM), scalar1=-1.0, scalar2=None, op0=ALU.mult), ["sc"], ["sc"])
    vop(lambda e: e.tensor_tensor(out=scol(DEN), in0=are, in1=are, op=ALU.mult), ["pp"], ["sc"])
    vop(lambda e: e.tensor_tensor(out=scol(X1), in0=aim, in1=aim, op=ALU.mult), ["pp", "sc"], ["sc"])
    tt_(DEN, DEN, X1, ALU.add)
    vop(lambda e: e.reciprocal(out=scol(DEN), in_=scol(DEN)), ["sc"], ["sc"])
    vop(lambda e: e.tensor_tensor(out=scol(X1), in0=scol(NRE), in1=are, op=ALU.mult), ["pp", "sc"], ["sc"])
    vop(lambda e: e.tensor_tensor(out=scol(X2), in0=scol(LIM), in1=aim, op=ALU.mult), ["pp", "sc"], ["sc"])
    tt_(X1, X1, X2, ALU.add)
    tt_(FRE, X1, DEN, ALU.mult)
    vop(lambda e: e.tensor_tensor(out=scol(X1), in0=scol(LIM), in1=are, op=ALU.mult), ["pp", "sc"], ["sc"])
    vop(lambda e: e.tensor_tensor(out=scol(X2), in0=scol(NRE), in1=aim, op=ALU.mult), ["pp", "sc"], ["sc"])
    tt_(X1, X1, X2, ALU.subtract)
    tt_(FIM, X1, DEN, ALU.mult)
    fre_b = sc[:, FRE, :, None].broadcast_to([128, 4, 16])
    fim_b = sc[:, FIM, :, None].broadcast_to([128, 4, 16])
    t16 = T[4][:, 0:64].rearrange("p (m c) -> p m c", m=4)
    vop(lambda e: e.tensor_tensor(out=bbr[:, 0], in0=bb[:, 0], in1=fre_b, op=ALU.mult), ["bb", "sc"], ["bbr"])
    vop(lambda e: e.tensor_tensor(out=t16, in0=bb[:, 1], in1=fim_b, op=ALU.mult), ["bb", "sc"], ["t16"])
    vop(lambda e: e.tensor_tensor(out=bbr[:, 0], in0=bbr[:, 0], in1=t16, op=ALU.subtract), ["bbr", "t16"], ["bbr"])
    vop(lambda e: e.tensor_tensor(out=bbr[:, 1], in0=bb[:, 1], in1=fre_b, op=ALU.mult), ["bb", "sc", "bbr"], ["bbr"])
    vop(lambda e: e.tensor_tensor(out=t16, in0=bb[:, 0], in1=fim_b, op=ALU.mult), ["bb", "sc", "t16"], ["t16"])
    vop(lambda e: e.tensor_tensor(out=bbr[:, 1], in0=bbr[:, 1], in1=t16, op=ALU.add), ["bbr", "t16"], ["bbr"])
    identf = self.cmf[:, 2, :]
    bki = [0]

    def tbank():
        i = 4 + bki[0] % 4
        bki[0] += 1
        return banks[i], f"bk{i}"

    for m in range(4):
        for ri in range(2):
            src = bbr[:, ri, m, None, :].broadcast_to([128, 8, 16])
            vop(lambda e, src=src, m=m: e.tensor_tensor(out=wtmp[:].rearrange("p (a b) -> p a b", a=8), in0=kk[:, m, :].rearrange("p (a b) -> p a b", a=8), in1=src, op=ALU.mult), ["kk", "bbr", "wtmp"], ["wtmp"])
            pb, pk = tbank()
            P.op("tensor", lambda e, pb=pb: e.transpose(out=pb[:, 0:128], in_=wtmp[:], identity=identf), reads=["wtmp", "cmf"], writes=[pk])
            vop(lambda e, pb=pb, ri=ri, m=m: e.tensor_copy(out=lB[:, ri, m, :], in_=pb[:, 0:128]), [pk], ["lB"])
        for ri in range(2):
            vop(lambda e, ri=ri, m=m: e.tensor_tensor(out=wtmp[:], in0=kk[:, 4 + m, :], in1=cc[:, ri].rearrange("p a b -> p (a b)"), op=ALU.mult), ["kk", "cc", "wtmp"], ["wtmp"])
            pb, pk = tbank()
            P.op("tensor", lambda e, pb=pb: e.transpose(out=pb[:, 0:128], in_=wtmp[:], identity=identf), reads=["wtmp", "cmf"], writes=[pk])
            if ri == 0:
                vop(lambda e, pb=pb, m=m: e.tensor_copy(out=lC[:, 0, m, :], in_=pb[:, 0:128]), [pk], ["lC"])
                vop(lambda e, pb=pb, m=m: e.tensor_scalar(out=lC[:, 1, m, :], in0=pb[:, 0:128], scalar1=-1.0, scalar2=None, op0=ALU.mult), [pk], ["lC"])
            else:
                vop(lambda e, pb=pb, m=m: e.tensor_scalar(out=lC[:, 2, m, :], in0=pb[:, 0:128], scalar1=-1.0, scalar2=None, op0=ALU.mult), [pk], ["lC"])
    for m in range(4):
        vop(lambda e, m=m: e.tensor_scalar(out=T[0][:, m * 128:(m + 1) * 128], in0=iota, scalar1=sc[:, TH, m:m + 1], scalar2=None, op0=ALU.mult), ["kk", "sc"], ["rrs_ang"])
        vop(lambda e, m=m: e.activation(out=RP[:, m, :], in_=iota, func=AF.Exp, scale=sc[:, ADT, m:m + 1]), ["kk", "sc"], ["RP"], A_)
    range_reduce_sin(P, V_, T[1][:], T[0][:], T[2][:], ki[:], T[3][:], "rrs_", shift=0.0)
    vop(lambda e: e.tensor_copy(out=sinT[:].rearrange("p m j -> p (m j)"), in_=T[1][:]), ["rrs_r"], ["sinT"])
    range_reduce_sin(P, V_, T[1][:], T[0][:], T[2][:], ki[:], T[3][:], "rrs_", shift=PI / 2)
    vop(lambda e: e.tensor_copy(out=cosT[:].rearrange("p m j -> p (m j)"), in_=T[1][:]), ["rrs_r"], ["cosT"])
    vop(lambda e: e.tensor_copy(out=scol(C127), in_=cosT[:, :, 127]), ["cosT"], ["sc"])
    vop(lambda e: e.tensor_copy(out=scol(S127), in_=sinT[:, :, 127]), ["sinT"], ["sc"])
    vop(lambda e: e.tensor_scalar(out=scol(NS127), in0=scol(S127), scalar1=-1.0, scalar2=None, op0=ALU.mult), ["sc"], ["sc"])
    vop(lambda e: e.tensor_tensor(out=scol(R128), in0=RP[:, :, 127], in1=scol(MAG), op=ALU.mult), ["RP", "sc"], ["sc"])
    tt_(X1, C127, CS, ALU.mult)
    tt_(X2, S127, SN, ALU.mult)
    tt_(C128, X1, X2, ALU.subtract)
    tt_(X1, S127, CS, ALU.mult)
    tt_(X2, C127, SN, ALU.mult)
    tt_(S128, X1, X2, ALU.add)
    tt_(AK, R128, C128, ALU.mult)
    tt_(AK + 1, R128, S128, ALU.mult)
    for k in range(6):
        vop(lambda e, k=k: e.tensor_scalar(out=scol(AK + 12 + k), in0=scol(AK + 2 * k + 1), scalar1=-1.0, scalar2=None, op0=ALU.mult), ["sc"], ["sc"])
        if k < 5:
            tt_(X1, AK + 2 * k, AK + 2 * k, ALU.mult)
            tt_(X2, AK + 2 * k + 1, AK + 2 * k + 1, ALU.mult)
            tt_(AK + 2 * k + 2, X1, X2, ALU.subtract)
            tt_(X1, AK + 2 * k, AK + 2 * k + 1, ALU.mult)
            vop(lambda e, k=k: e.tensor_scalar(out=scol(AK + 2 * k + 3), in0=scol(X1), scalar1=2.0, scalar2=None, op0=ALU.mult), ["sc"], ["sc"])

    ti = [0]

    def do_tile(m):
        cosb = cosT[:, m, None, :].broadcast_to([128, 4, 128])
        sinb = sinT[:, m, None, :].broadcast_to([128, 4, 128])
        rpb = RP[:, m, None, :].broadcast_to([128, 4, 128])
        v4 = lambda ap: ap.rearrange("p (c j) -> p c j", c=4)
        vop(lambda e, m=m: e.tensor_copy(out=MUL[:], in_=sc[:, MAG, m:m + 1].broadcast_to([128, 2048])), ["sc", "MUL"], ["MUL"])
        vop(lambda e: e.memset(MUL[:, 0:2048:128], 0.0), ["MUL"], ["MUL"])
        for Tt in range(16):
            sl = slice(Tt * 512, (Tt + 1) * 512)
            pa, pak = tbank()
            pb, pbk = tbank()
            P.op("tensor", lambda e, pa=pa, m=m, sl=sl: e.matmul(out=pa[:], lhsT=lB[:, 0, m, :], rhs=CU[:, sl], start=True, stop=True), reads=["lB"] + allr3, writes=[pak])
            P.op("tensor", lambda e, pb=pb, m=m, sl=sl: e.matmul(out=pb[:], lhsT=lB[:, 1, m, :], rhs=CU[:, sl], start=True, stop=True), reads=["lB"] + allr3, writes=[pbk])
            j = (ti[0] % 2) * 4
            ti[0] += 1
            t1, t2, t3, t4 = T[j], T[j + 1], T[j + 2], T[j + 3]
            k1, k2, k3, k4 = [f"T{j + q}" for q in range(4)]
            vop(lambda e, t1=t1, pa=pa: e.tensor_tensor(out=v4(t1[:]), in0=v4(pa[:]), in1=cosb, op=ALU.mult), [pak, "cosT"], [k1])
            vop(lambda e, t2=t2, pb=pb: e.tensor_tensor(out=v4(t2[:]), in0=v4(pb[:]), in1=sinb, op=ALU.mult), [pbk, "sinT"], [k2])
            vop(lambda e, t3=t3, pb=pb: e.tensor_tensor(out=v4(t3[:]), in0=v4(pb[:]), in1=cosb, op=ALU.mult), [pbk, "cosT"], [k3])
            vop(lambda e, t4=t4, pa=pa: e.tensor_tensor(out=v4(t4[:]), in0=v4(pa[:]), in1=sinb, op=ALU.mult), [pak, "sinT"], [k4])
            vop(lambda e, t1=t1, t2=t2, sl=sl: e.tensor_tensor(out=WR[:, sl], in0=t1[:], in1=t2[:], op=ALU.add), [k1, k2], ["R0"], G_)
            vop(lambda e, t3=t3, t4=t4, sl=sl: e.tensor_tensor(out=WI[:, sl], in0=t3[:], in1=t4[:], op=ALU.subtract), [k3, k4], ["R1"], G_)
        for W_, wk in ((WR, "R0"), (WI, "R1")):
            for q in range(4):
                sl = slice(q * 2048, (q + 1) * 2048)
                vop(lambda e, W_=W_, sl=sl: e.tensor_tensor_scan(out=W_[:, sl], data0=MUL[:], data1=W_[:, sl], initial=0.0, op0=ALU.mult, op1=ALU.add), [wk, "MUL"], [wk])
        Lr = WR[:, 127:SEQ:128]
        Li = WI[:, 127:SEQ:128]
        c127, s127, ns127 = sc[:, C127, m:m + 1], sc[:, S127, m:m + 1], sc[:, NS127, m:m + 1]
        Xr, Xi = L2[:, 0, :], L2[:, 1, :]
        vop(lambda e: e.tensor_scalar(out=Xr, in0=Lr, scalar1=c127, scalar2=None, op0=ALU.mult), ["R0", "sc"], ["L2"])
        vop(lambda e: e.scalar_tensor_tensor(out=Xr, in0=Li, scalar=ns127, in1=Xr, op0=ALU.mult, op1=ALU.add), ["R1", "sc", "L2"], ["L2"])
        vop(lambda e: e.tensor_scalar(out=Xi, in0=Li, scalar1=c127, scalar2=None, op0=ALU.mult), ["R1", "sc", "L2"], ["L2"])
        vop(lambda e: e.scalar_tensor_tensor(out=Xi, in0=Lr, scalar=s127, in1=Xi, op0=ALU.mult, op1=ALU.add), ["R0", "sc", "L2"], ["L2"])
        cur = 0
        for k in range(6):
            s_ = 1 << k
            n_ = 64 - s_
            o_r, o_i = L2[:, 2 * cur, :], L2[:, 2 * cur + 1, :]
            nxt = 1 - cur
            n_r, n_i = L2[:, 2 * nxt, :], L2[:, 2 * nxt + 1, :]
            tr, tim = L2[:, 4, :], L2[:, 5, :]
            akr, aki, naki = sc[:, AK + 2 * k, m:m + 1], sc[:, AK + 2 * k + 1, m:m + 1], sc[:, AK + 12 + k, m:m + 1]
            vop(lambda e, n_r=n_r, o_r=o_r, s_=s_: e.tensor_copy(out=n_r[:, 0:s_], in_=o_r[:, 0:s_]), ["L2"], ["L2"])
            vop(lambda e, n_i=n_i, o_i=o_i, s_=s_: e.tensor_copy(out=n_i[:, 0:s_], in_=o_i[:, 0:s_]), ["L2"], ["L2"])
            vop(lambda e, tr=tr, o_r=o_r, n_=n_, akr=akr: e.tensor_scalar(out=tr[:, 0:n_], in0=o_r[:, 0:n_], scalar1=akr, scalar2=None, op0=ALU.mult), ["L2", "sc"], ["L2"])
            vop(lambda e, tr=tr, o_i=o_i, n_=n_, naki=naki: e.scalar_tensor_tensor(out=tr[:, 0:n_], in0=o_i[:, 0:n_], scalar=naki, in1=tr[:, 0:n_], op0=ALU.mult, op1=ALU.add), ["L2", "sc"], ["L2"])
            vop(lambda e, tr=tr, o_r=o_r, n_r=n_r, n_=n_, s_=s_: e.tensor_tensor(out=n_r[:, s_:64], in0=o_r[:, s_:64], in1=tr[:, 0:n_], op=ALU.add), ["L2"], ["L2"])
            vop(lambda e, tim=tim, o_i=o_i, n_=n_, akr=akr: e.tensor_scalar(out=tim[:, 0:n_], in0=o_i[:, 0:n_], scalar1=akr, scalar2=None, op0=ALU.mult), ["L2", "sc"], ["L2"])
            vop(lambda e, tim=tim, o_r=o_r, n_=n_, aki=aki: e.scalar_tensor_tensor(out=tim[:, 0:n_], in0=o_r[:, 0:n_], scalar=aki, in1=tim[:, 0:n_], op0=ALU.mult, op1=ALU.add), ["L2", "sc"], ["L2"])
            vop(lambda e, tim=tim, o_i=o_i, n_i=n_i, n_=n_, s_=s_: e.tensor_tensor(out=n_i[:, s_:64], in0=o_i[:, s_:64], in1=tim[:, 0:n_], op=ALU.add), ["L2"], ["L2"])
            cur = nxt
        XrF, XiF = L2[:, 2 * cur, :], L2[:, 2 * cur + 1, :]
        Kr, Ki = L2[:, 6, :], L2[:, 7, :]
        lre, lim, nlim = sc[:, LRE, m:m + 1], sc[:, LIM, m:m + 1], sc[:, NLIM, m:m + 1]
        vop(lambda e: e.tensor_scalar(out=Kr, in0=XrF, scalar1=lre, scalar2=None, op0=ALU.mult), ["L2", "sc"], ["L2"])
        vop(lambda e: e.scalar_tensor_tensor(out=Kr, in0=XiF, scalar=nlim, in1=Kr, op0=ALU.mult, op1=ALU.add), ["L2", "sc"], ["L2"])
        vop(lambda e: e.tensor_scalar(out=Ki, in0=XiF, scalar1=lre, scalar2=None, op0=ALU.mult), ["L2", "sc"], ["L2"])
        vop(lambda e: e.scalar_tensor_tensor(out=Ki, in0=XrF, scalar=lim, in1=Ki, op0=ALU.mult, op1=ALU.add), ["L2", "sc"], ["L2"])
        for Tt in range(16):
            for W_, wk, Kx in ((WR, "R0", Kr), (WI, "R1", Ki)):
                c0 = Tt * 4
                tmp = T[ti[0] % 8]
                tk = f"T{ti[0] % 8}"
                ti[0] += 1
                if Tt == 0:
                    vop(lambda e, tmp=tmp, Kx=Kx: e.memset(tmp[:, 0:128], 0.0), [tk], [tk], G_)
                    vop(lambda e, tmp=tmp, Kx=Kx: e.tensor_tensor(out=tmp[:, 128:512].rearrange("p (c j) -> p c j", c=3), in0=RP[:, m, None, :].broadcast_to([128, 3, 128]), in1=Kx[:, 0:3, None].broadcast_to([128, 3, 128]), op=ALU.mult), ["RP", "L2", tk], [tk], G_)
                else:
                    vop(lambda e, tmp=tmp, Kx=Kx, c0=c0: e.tensor_tensor(out=v4(tmp[:]), in0=rpb, in1=Kx[:, c0 - 1:c0 + 3, None].broadcast_to([128, 4, 128]), op=ALU.mult), ["RP", "L2", tk], [tk], G_)
                sl = slice(Tt * 512, (Tt + 1) * 512)
                vop(lambda e, tmp=tmp, W_=W_, sl=sl: e.tensor_tensor(out=W_[:, sl], in0=W_[:, sl], in1=tmp[:], op=ALU.add), [tk, wk], [wk], G_)
        for Tt in range(16):
            sl = slice(Tt * 512, (Tt + 1) * 512)
            j = (ti[0] % 2) * 4
            ti[0] += 1
            vs = [VV[j + q] for q in range(4)]
            vk = [f"V{j + q}" for q in range(4)]
            vop(lambda e, v=vs[0], sl=sl: e.tensor_tensor(out=v4(v[:]), in0=v4(WR[:, sl]), in1=cosb, op=ALU.mult), ["R0", "cosT", vk[0]], [vk[0]])
            vop(lambda e, v=vs[1], sl=sl: e.tensor_tensor(out=v4(v[:]), in0=v4(WI[:, sl]), in1=sinb, op=ALU.mult), ["R1", "sinT", vk[1]], [vk[1]], G_)
            vop(lambda e, v=vs[2], sl=sl: e.tensor_tensor(out=v4(v[:]), in0=v4(WR[:, sl]), in1=sinb, op=ALU.mult), ["R0", "sinT", vk[2]], [vk[2]])
            vop(lambda e, v=vs[3], sl=sl: e.tensor_tensor(out=v4(v[:]), in0=v4(WI[:, sl]), in1=cosb, op=ALU.mult), ["R1", "cosT", vk[3]], [vk[3]], G_)
            pb, pk = tbank()
            for q, li in enumerate((0, 1, 2, 2)):
                P.op("tensor", lambda e, pb=pb, q=q, li=li, m=m, v=vs[q]: e.matmul(out=pb[:], lhsT=lC[:, li, m, :], rhs=v[:], start=(q == 0), stop=(q == 3)),
                     reads=["lC", vk[q]], writes=[pk])
            if m == SSM_MS[0]:
                vop(lambda e, pb=pb, sl=sl: e.activation(out=Y[:, sl], in_=pb[:], func=AF.Copy), [pk], [ykey(Tt)], A_)
            elif not SKIP_ADD:
                vop(lambda e, pb=pb, sl=sl: e.tensor_tensor(out=Y[:, sl], in0=Y[:, sl], in1=pb[:], op=ALU.add), [pk, ykey(Tt)], [ykey(Tt)])
    for m in SSM_MS:
        do_tile(m)
    for Tt in range(16):
        sl = slice(Tt * 512, (Tt + 1) * 512)
        j = (ti[0] % 2) * 4
        ti[0] += 1
        yv, y2, sg = T[j], T[j + 1], T[j + 2]
        ky, k2, kg = f"T{j}", f"T{j + 1}", f"T{j + 2}"
        vop(lambda e, yv=yv, sl=sl: e.scalar_tensor_tensor(out=yv[:], in0=CU[:, sl], scalar=dcol, in1=Y[:, sl], op0=ALU.mult, op1=ALU.add), allr3 + ["pp", ykey(Tt), ky], [ky])
        vop(lambda e, yv=yv, y2=y2: e.tensor_tensor(out=y2[:], in0=yv[:], in1=yv[:], op=ALU.mult), [ky, k2], [k2], G_)
        vop(lambda e, y2=y2: e.tensor_scalar(out=y2[:], in0=y2[:], scalar1=0.044715, scalar2=1.0, op0=ALU.mult, op1=ALU.add), [k2], [k2], G_)
        vop(lambda e, yv=yv, y2=y2: e.tensor_tensor(out=y2[:], in0=y2[:], in1=yv[:], op=ALU.mult), [ky, k2], [k2], G_)
        vop(lambda e, y2=y2, sg=sg: e.activation(out=sg[:], in_=y2[:], func=AF.Sigmoid, scale=1.5957691216057308), [k2, kg], [kg], A_)
        os_ = self.ost[self.ost_i % 2]
        osk = f"ost{self.ost_i % 2}"
        vop(lambda e, yv=yv, sg=sg, os_=os_, Tt=Tt: e.tensor_tensor(out=os_[:, (Tt % 4) * 512:(Tt % 4 + 1) * 512], in0=yv[:], in1=sg[:], op=ALU.mult), [ky, kg, osk], [osk])
        if Tt % 4 == 3:
            P.dma(S, lambda e, os_=os_, Tt=Tt: e.dma_start(out=self.Z[:, (Tt - 3) * 512:(Tt + 1) * 512], in_=os_[:]), reads=[osk], is_output=True)
            self.ost_i += 1


PhaseB.ssm = _ssm


_PROGS = {}


def _prog(name):
    if name not in _PROGS:
        if name == "A":
            _PROGS[name] = build_phaseA()
        elif name == "B":
            _PROGS[name] = PhaseB().close()
        else:
            _PROGS[name] = build_phaseC()
    return _PROGS[name]


def _run(name, maps):
    res = run_bass_kernel_spmd(_prog(name), maps, core_ids=list(range(8)))
    return res.results


def kernel(**inp):
    inp = {k: np.asarray(v) for k, v in inp.items()}
    x = inp["x"].astype(np.float32)
    pos = inp["positions"].astype(np.int32)
    xT = [np.ascontiguousarray(x[b].T) for b in range(NB)]
    cores = [(b, j) for b in range(NB) for j in range(4)]
    for l in range(DEPTH):
        mapsA = [phaseA_inputs(xT[b][:, j * TOK:(j + 1) * TOK], pos[b, j * TOK:(j + 1) * TOK], l, inp) for (b, j) in cores]
        rA = _run("A", mapsA)
        QK = [np.concatenate([rA[b * 4 + j]["QK"] for j in range(4)], axis=2) for b in range(NB)]
        Vv = [np.concatenate([rA[b * 4 + j]["V"] for j in range(4)], axis=0) for b in range(NB)]
        CUv = [np.concatenate([rA[b * 4 + j]["CU"] for j in range(4)], axis=2) for b in range(NB)]
        Gv = [np.concatenate([rA[b * 4 + j]["G"] for j in range(4)], axis=2) for b in range(NB)]
        mapsB = []
        for (b, h) in cores:
            m = {
                "QA": np.ascontiguousarray(np.stack([QK[b][h], QK[b][4 + h]])),
                "VA": np.ascontiguousarray(Vv[b][:, 128 * h:128 * h + 128]),
                "QB": np.ascontiguousarray(np.stack([np.stack([QK[b][8 + g * 4 + h], QK[b][20 + g * 4 + h]]) for g in range(3)])),
                "VB": np.ascontiguousarray(np.stack([Vv[b][:, 512 + (g * 4 + h) * 128:512 + (g * 4 + h + 1) * 128] for g in range(3)])),
                "CU": np.ascontiguousarray(CUv[b][h]),
            }
            m.update(phaseB_param_inputs(l, h, inp))
            mapsB.append(m)
        rB = _run("B", mapsB)
        Wc = phaseC_weight_inputs(l, inp)
        mapsC = []
        for (b, j) in cores:
            O = np.stack([rB[b * 4 + h][nm] for nm in ("OA", "OB", "Z") for h in range(4)])
            m = dict(Wc)
            m["xT"] = make_units(xT[b], j)
            m["O"] = make_units(O, j)
            m["G"] = make_units(Gv[b], j)
            hv = np.ones((128, 2), np.float32)
            if j == 0:
                hv[:, 0] = 0.0
            m["hv"] = hv
            mapsC.append(m)
        rC = _run("C", mapsC)
        xT = [np.ascontiguousarray(np.concatenate([np.concatenate([rC[b * 4 + j]["Xo"][0], rC[b * 4 + j]["Xo"][1]], axis=1) for j in range(4)], axis=1)) for b in range(NB)]
    out = np.stack([xT[b].T for b in range(NB)]).astype(np.float32)
    return np.ascontiguousarray(out)
```
